# Optimizing a Trainium2 kernel written in Bass

```python
import math
import jax, jax.numpy as jnp
from jax import lax
import numpy as np

D_MODEL = 1024
BATCH = 8
SEQ = 8192
DEPTH = 1
DEC_BATCH = 8
DEC_SEQ = 32
PAST_LEN = 4096

CHUNK = 64
Q_BLOCK = 128
ATTN_WIDTH = D_MODEL // 2
POOL_WIDTH = D_MODEL - ATTN_WIDTH
N_HEADS = 8
V_HEAD_DIM = ATTN_WIDTH // N_HEADS
QK_NOPE_DIM = 64
QK_ROPE_DIM = 32
Q_LORA_RANK = 256
KV_LORA_RANK = 128
ROPE_BASE = 10000.0
POOL_WINDOWS = (2, 4, 8, 16)
N_POOL_GROUPS = len(POOL_WINDOWS)
POOL_GROUP_DIM = POOL_WIDTH // N_POOL_GROUPS
POOL_STATE = max(POOL_WINDOWS) - 1
D_FF = -(-8 * D_MODEL // (3 * 256)) * 256
IN_WIDTH = Q_LORA_RANK + KV_LORA_RANK + QK_ROPE_DIM + POOL_WIDTH
SM_SCALE = 1.0 / math.sqrt(QK_NOPE_DIM + QK_ROPE_DIM)
EPS = 1e-6

kernel_name = "hymba_mla_pool_streaming_step"


def rmsnorm(x, g):
    xf = x.astype(jnp.float32)
    y = xf * lax.rsqrt(jnp.mean(xf * xf, axis=-1, keepdims=True) + EPS)
    return (y * g.astype(jnp.float32)).astype(x.dtype)


def rope(x, pos):
    d = x.shape[-1]
    freqs = jnp.power(ROPE_BASE, -jnp.arange(0, d, 2, dtype=jnp.float32) / d)
    ang = pos.astype(jnp.float32)[:, None] * freqs[None, :]
    cos = jnp.cos(ang)[None, :, None, :]
    sin = jnp.sin(ang)[None, :, None, :]
    xf = x.astype(jnp.float32)
    x1, x2 = xf[..., : d // 2], xf[..., d // 2:]
    return jnp.concatenate([x1 * cos - x2 * sin, x2 * cos + x1 * sin], axis=-1).astype(x.dtype)


def attend(q_lat, q_rope, keys_lat, keys_rope, q_pos, k_pos):
    s = (jnp.einsum('bqhr,bkr->bhqk', q_lat, keys_lat)
         + jnp.einsum('bqhd,bkd->bhqk', q_rope, keys_rope)).astype(jnp.float32) * SM_SCALE
    mask = (k_pos[None, :] // CHUNK) <= (q_pos[:, None] // CHUNK)
    s = jnp.where(mask[None, None], s, -jnp.inf)
    p = jax.nn.softmax(s, axis=-1).astype(keys_lat.dtype)
    return jnp.einsum('bhqk,bkr->bqhr', p, keys_lat)


def pool_mix(u, prev, w_pool, pool_scale, pos):
    B, T, _ = u.shape
    ext = jnp.concatenate([prev, u], axis=1)
    extf = ext.astype(jnp.float32)
    cs = jnp.concatenate([jnp.zeros((B, 1, POOL_WIDTH), jnp.float32),
                          jnp.cumsum(extf, axis=1)], axis=1)
    hi = cs[:, POOL_STATE + 1:]
    outs = []
    for g, w in enumerate(POOL_WINDOWS):
        sl = slice(g * POOL_GROUP_DIM, (g + 1) * POOL_GROUP_DIM)
        lo = cs[:, POOL_STATE + 1 - w: POOL_STATE + 1 - w + T, sl]
        cnt = jnp.minimum(pos + 1, w).astype(jnp.float32)[None, :, None]
        outs.append((hi[..., sl] - lo) / cnt - extf[:, POOL_STATE:, sl])
    pooled = jnp.stack(outs, axis=2).astype(u.dtype)
    mixed = jnp.einsum('btgc,gcd->btgd', pooled, w_pool).reshape(B, T, POOL_WIDTH)
    return mixed * pool_scale, ext[:, -POOL_STATE:]


def layer(x, pos, ckv_prev, kr_prev, pool_prev, g_norm1, w_in, g_q, w_uq, g_kv, w_uk, w_uv,
          w_pool, pool_scale, g_out_attn, g_out_pool, w_o, g_norm2, w_gate, w_up, w_down, blocked):
    B, T, _ = x.shape
    h = rmsnorm(x, g_norm1)
    z = h @ w_in
    a0 = Q_LORA_RANK
    a1 = a0 + KV_LORA_RANK
    a2 = a1 + QK_ROPE_DIM
    c_q, c_kv, k_r, u = z[..., :a0], z[..., a0:a1], z[..., a1:a2], z[..., a2:]
    q = (rmsnorm(c_q, g_q) @ w_uq).reshape(B, T, N_HEADS, QK_NOPE_DIM + QK_ROPE_DIM)
    q_nope, q_rope = q[..., :QK_NOPE_DIM], rope(q[..., QK_NOPE_DIM:], pos)
    q_lat = jnp.einsum('bthd,hrd->bthr', q_nope, w_uk)
    c_kv = rmsnorm(c_kv, g_kv)
    k_r = rope(k_r[:, :, None, :], pos)[:, :, 0, :]
    if ckv_prev is None:
        keys_lat, keys_rope, k_pos = c_kv, k_r, pos
    else:
        keys_lat = jnp.concatenate([ckv_prev, c_kv], axis=1)
        keys_rope = jnp.concatenate([kr_prev, k_r], axis=1)
        k_pos = jnp.concatenate([jnp.arange(ckv_prev.shape[1]), pos])
    if blocked:
        nb = T // Q_BLOCK
        ql_b = q_lat.reshape(B, nb, Q_BLOCK, N_HEADS, KV_LORA_RANK).transpose(1, 0, 2, 3, 4)
        qr_b = q_rope.reshape(B, nb, Q_BLOCK, N_HEADS, QK_ROPE_DIM).transpose(1, 0, 2, 3, 4)

        def blk(args):
            i, ql, qr = args
            qp = lax.dynamic_slice(pos, (i * Q_BLOCK,), (Q_BLOCK,))
            return attend(ql, qr, keys_lat, keys_rope, qp, k_pos)

        o_b = lax.map(blk, (jnp.arange(nb), ql_b, qr_b))
        o_lat = o_b.transpose(1, 0, 2, 3, 4).reshape(B, T, N_HEADS, KV_LORA_RANK)
    else:
        o_lat = attend(q_lat, q_rope, keys_lat, keys_rope, pos, k_pos)
    o_attn = jnp.einsum('bthr,hrd->bthd', o_lat, w_uv).reshape(B, T, ATTN_WIDTH)
    if pool_prev is None:
        pool_prev = jnp.zeros((B, POOL_STATE, POOL_WIDTH), u.dtype)
    o_pool, new_pool = pool_mix(u, pool_prev, w_pool, pool_scale, pos)
    mix = jnp.concatenate([rmsnorm(o_attn, g_out_attn), rmsnorm(o_pool, g_out_pool)], axis=-1) @ w_o
    x = x + mix
    h2 = rmsnorm(x, g_norm2)
    x = x + (jax.nn.silu(h2 @ w_gate) * (h2 @ w_up)) @ w_down
    return x, c_kv, k_r, new_pool


def setup_inputs(seed: int = 0) -> dict:
    key = jax.random.key(seed)
    ks = jax.random.split(key, 24)
    f32 = jnp.float32

    def nrm(k, shape, scale):
        return jax.random.normal(k, shape, f32) * scale

    def gain(k, shape):
        return 1.0 + 0.05 * jax.random.normal(k, shape, f32)

    L = DEPTH
    return {
        "x_prompt": nrm(ks[0], (BATCH, SEQ, D_MODEL), 1.0),
        "x_sample": nrm(ks[1], (DEC_BATCH, DEC_SEQ, D_MODEL), 1.0),
        "cache_kv_latent": nrm(ks[2], (L, DEC_BATCH, PAST_LEN, KV_LORA_RANK), 1.0),
        "cache_k_rope": nrm(ks[3], (L, DEC_BATCH, PAST_LEN, QK_ROPE_DIM), 1.0),
        "state_pool": nrm(ks[4], (L, DEC_BATCH, POOL_STATE, POOL_WIDTH), 1.0),
        "g_norm1": gain(ks[5], (L, D_MODEL)),
        "w_in": nrm(ks[6], (L, D_MODEL, IN_WIDTH), D_MODEL ** -0.5),
        "g_q": gain(ks[7], (L, Q_LORA_RANK)),
        "w_uq": nrm(ks[8], (L, Q_LORA_RANK, N_HEADS * (QK_NOPE_DIM + QK_ROPE_DIM)), Q_LORA_RANK ** -0.5),
        "g_kv": gain(ks[9], (L, KV_LORA_RANK)),
        "w_uk": nrm(ks[10], (L, N_HEADS, KV_LORA_RANK, QK_NOPE_DIM), KV_LORA_RANK ** -0.5),
        "w_uv": nrm(ks[11], (L, N_HEADS, KV_LORA_RANK, V_HEAD_DIM), KV_LORA_RANK ** -0.5),
        "w_pool": nrm(ks[12], (L, N_POOL_GROUPS, POOL_GROUP_DIM, POOL_GROUP_DIM), POOL_GROUP_DIM ** -0.5),
        "pool_scale": gain(ks[13], (L, POOL_WIDTH)),
        "g_out_attn": gain(ks[14], (L, ATTN_WIDTH)),
        "g_out_pool": gain(ks[15], (L, POOL_WIDTH)),
        "w_o": nrm(ks[16], (L, D_MODEL, D_MODEL), D_MODEL ** -0.5),
        "g_norm2": gain(ks[17], (L, D_MODEL)),
        "w_gate": nrm(ks[18], (L, D_MODEL, D_FF), D_MODEL ** -0.5),
        "w_up": nrm(ks[19], (L, D_MODEL, D_FF), D_MODEL ** -0.5),
        "w_down": nrm(ks[20], (L, D_FF, D_MODEL), D_FF ** -0.5),
        "g_final": gain(ks[21], (D_MODEL,)),
    }


def reference(x_prompt, x_sample, cache_kv_latent, cache_k_rope, state_pool,
              g_norm1, w_in, g_q, w_uq, g_kv, w_uk, w_uv, w_pool, pool_scale,
              g_out_attn, g_out_pool, w_o, g_norm2, w_gate, w_up, w_down, g_final):
    past = cache_kv_latent.shape[2]
    pos_p = jnp.arange(x_prompt.shape[1])
    pos_s = past + jnp.arange(x_sample.shape[1])
    xp, xs = x_prompt, x_sample
    ckv_p, kr_p, pool_p, ckv_s, kr_s, pool_s = [], [], [], [], [], []
    for l in range(DEPTH):
        w = (g_norm1[l], w_in[l], g_q[l], w_uq[l], g_kv[l], w_uk[l], w_uv[l], w_pool[l], pool_scale[l],
             g_out_attn[l], g_out_pool[l], w_o[l], g_norm2[l], w_gate[l], w_up[l], w_down[l])
        xp, a, b, c = layer(xp, pos_p, None, None, None, *w, blocked=True)
        ckv_p.append(a); kr_p.append(b); pool_p.append(c)
        xs, a, b, c = layer(xs, pos_s, cache_kv_latent[l], cache_k_rope[l], state_pool[l], *w, blocked=False)
        ckv_s.append(a); kr_s.append(b); pool_s.append(c)
    y_prompt = rmsnorm(xp, g_final)
    y_sample = rmsnorm(xs, g_final)
    return (y_prompt, y_sample,
            jnp.stack(ckv_p), jnp.stack(kr_p), jnp.stack(pool_p),
            jnp.stack(ckv_s), jnp.stack(kr_s), jnp.stack(pool_s))
```

```python
import os
import math
from contextlib import ExitStack
import numpy as np
import concourse.bass as bass
import concourse.mybir as mybir
from concourse.bass_utils import run_bass_kernel_spmd

F32 = mybir.dt.float32
BF16 = mybir.dt.bfloat16
AF = mybir.ActivationFunctionType
ALU = mybir.AluOpType

D = 1024
SEQ = 8192
DEC = 32
PAST = 4096
NH = 8
R = 128
DN = 64
DR = 32
QL = 256
PW = 512
DFF = 2816
NF = DFF // 128
INW = 928
EPS = 1e-6
SM_SCALE = 1.0 / math.sqrt(DN + DR)
WINDOWS = (2, 4, 8, 16)
PI = math.pi
EXT = 16
UW = EXT + 512

NBLK = 16
DO_SAMPLE = 1
STAGE = 99
NCORES = 8
SUB = 99
VV = 99
POOL_ACC = 0
USE_ACCUM = 1
LNEXP = 1
EARLY_NORM = 1
POOL_MIX = 1
SPLIT_NORM2 = 1
STAGE_X = 1
EARLY_ROPE = 1


class _Op:
    __slots__ = ("eng", "fn", "deps", "sig", "cnt", "dkey", "dcnt", "idx")


class Tracker:
    ENGS = ("pe", "act", "dve", "pool", "sp")

    def __init__(self):
        self.ops = []
        self.lw = {}
        self.rd = {}
        self.eng_ops = {e: [] for e in self.ENGS}
        self.bulk = set()
        self.last_dma = {}

    def add(self, eng, fn, r=(), w=(), dkey=None, extra=()):
        op = _Op()
        op.eng = eng
        op.fn = fn
        op.dkey = dkey
        op.sig = False
        op.cnt = 0
        op.dcnt = 0
        deps = {}

        def need(p):
            if p is None:
                return
            if p.dkey is None and dkey is None and p.eng == "pe" and eng == "pe":
                return
            k = p.dkey if p.dkey is not None else p.eng
            q = deps.get(k)
            if q is None or q.idx < p.idx:
                deps[k] = p

        for x in r:
            need(self.lw.get(x))
            if x.startswith("ps"):
                rr = self.rd.get(x)
                if rr:
                    for k_, p in rr.items():
                        if k_ != (dkey if dkey is not None else eng):
                            need(p)
        for x in w:
            need(self.lw.get(x))
            rr = self.rd.get(x)
            if rr:
                for p in rr.values():
                    need(p)
        for p in extra:
            need(p)
        op.deps = list(deps.values())
        op.idx = len(self.ops)
        self.ops.append(op)
        self.eng_ops[eng].append(op)
        sk = dkey if dkey is not None else eng
        for x in r:
            d = self.rd.get(x)
            if d is None:
                d = self.rd[x] = {}
            d[sk] = op
        for x in w:
            self.lw[x] = op
            self.rd[x] = {}
        if dkey is not None:
            self.last_dma[dkey] = op
        return op

    def resolve(self):
        for op in self.ops:
            for p in op.deps:
                if p.dkey is None:
                    p.sig = True
        for e in self.ENGS:
            c = 0
            for op in self.eng_ops[e]:
                if op.dkey is None and op.sig:
                    c += 1
                    op.cnt = c
        tot = {}
        for op in self.ops:
            if op.dkey is not None:
                tot[op.dkey] = tot.get(op.dkey, 0) + 16
                op.dcnt = tot[op.dkey]
        for op in self.ops:
            if op.dkey in self.bulk:
                op.dcnt = tot[op.dkey]
        return sorted(tot.keys())

    def emit(self, eng_name, e, esem, dsem):
        known = {}
        for op in self.eng_ops[eng_name]:
            for p in op.deps:
                if p.dkey is not None:
                    sem, val, k = dsem[p.dkey], p.dcnt, ("d", p.dkey)
                else:
                    sem, val, k = esem[p.eng], p.cnt, ("e", p.eng)
                if known.get(k, 0) < val:
                    e.wait_ge(sem, val)
                    known[k] = val
            if op.fn is None:
                continue
            ins = op.fn(e)
            if op.dkey is not None:
                ins.then_inc(dsem[op.dkey], 16)
            elif op.sig:
                ins.then_inc(esem[op.eng], 1)


def build_program():
    nc = bass.Bass("TRN2", target_bir_lowering=False)
    T = Tracker()

    def din(name, shape, dt=F32):
        return nc.dram_tensor(name, list(shape), dt, kind="ExternalInput").ap()

    def dout(name, shape):
        return nc.dram_tensor(name, list(shape), F32, kind="ExternalOutput").ap()

    def dscr(name, shape, dt=BF16):
        return nc.dram_tensor(name, list(shape), dt, kind="Internal").ap()

    xp = din("xp", [SEQ, D])
    xsm = din("xsm", [DEC, D])
    cc_kv = din("cc_kv", [PAST, R])
    cc_kr = din("cc_kr", [PAST, DR])
    st_pool = din("st_pool", [15, PW])
    w_in_l = din("w_in_l", [128, 8 * INW])
    w_uq_l = din("w_uq_l", [128, 2 * 768])
    w_uk_l = din("w_uk_l", [128, NH * DN])
    w_uv_l = din("w_uv_l", [128, NH * DN])
    w_pool_l = din("w_pool_l", [128, 4 * 128])
    w_gu_l = din("w_gu_l", [NF * 128, 2 * 8 * 128])
    w_down_l = din("w_down_l", [NF * 128, D])
    w_o_l = din("w_o_l", [8 * 128, D])
    g1_l = din("g1_l", [128, 8])
    g2_l = din("g2_l", [128, 8])
    gq_l = din("gq_l", [128, 2])
    gkv_l = din("gkv_l", [128, 1])
    pscale_l = din("pscale_l", [128, 4])
    gop_l = din("gop_l", [128, 4])
    goa_l = din("goa_l", [128, 4])
    g_final = din("g_final", [D])
    c_ident = din("c_ident", [128, 128])
    c_iota = din("c_iota", [128, 512])
    c_freq = din("c_freq", [128, 1])
    c_invc = din("c_invc", [128, 64])
    c_mask = din("c_mask", [128, 4])

    y_p = dout("y_p", [SEQ, D])
    y_s = dout("y_s", [DEC, D])
    o_ckv_p = dout("o_ckv_p", [SEQ, R])
    o_kr_p = dout("o_kr_p", [SEQ, DR])
    o_pool_p = dout("o_pool_p", [15, PW])
    o_ckv_s = dout("o_ckv_s", [DEC, R])
    o_kr_s = dout("o_kr_s", [DEC, DR])
    o_pool_s = dout("o_pool_s", [15, PW])

    s_gu = dscr("s_gu", [NF * 128, 2 * 8 * 128])
    s_down = dscr("s_down", [NF * 128, D])
    s_wo = dscr("s_wo", [8 * 128, D])

    es = ExitStack()
    with es:
        def sb(name, shape, dt):
            return es.enter_context(nc.sbuf_tensor(name, list(shape), dt))

        def ps(name, shape, dt):
            return es.enter_context(nc.psum_tensor(name, list(shape), dt))

        KlatT = sb("KlatT", [128, SEQ], BF16)
        KropeT = sb("KropeT", [128, SEQ], BF16)
        Vt = sb("Vt", [128, SEQ // 128, R], BF16)
        win = sb("win", [128, 8, INW], BF16)
        wkr = sb("wkr", [128, 8, 128], BF16)
        wkrB = sb("wkrB", [128, 8, 128], BF16)
        wuqN = sb("wuqN", [128, 2, 512], BF16)
        wuqA = sb("wuqA", [128, 2, 256], BF16)
        wuqB = sb("wuqB", [128, 2, 256], BF16)
        wukT = sb("wukT", [128, 4, 128], BF16)
        wuvpad = sb("wuvpad", [128, NH, 128], BF16)
        wpool = sb("wpool", [128, 4, 128], BF16)
        gf_bc = sb("gf_bc", [128, D], F32)
        g1 = sb("g1", [128, 8], F32)
        g2 = sb("g2", [128, 8], F32)
        gq = sb("gq", [128, 2], F32)
        gkv = sb("gkv", [128, 1], F32)
        pscale = sb("pscale", [128, 4], F32)
        gop = sb("gop", [128, 4], F32)
        pg = sb("pg", [128, 4], F32)
        goa = sb("goa", [128, 4], F32)
        ident_f = sb("ident_f", [128, 128], F32)
        ident_b = sb("ident_b", [128, 128], BF16)
        ones_b = sb("ones_b", [128, 128], BF16)
        ones_f = sb("ones_f", [128, 128], F32)
        iota = sb("iota", [128, 512], F32)
        freq = sb("freq", [128, 1], F32)
        invc = sb("invc", [128, 4, 16], F32)
        maskc = sb("maskc", [128, 4], F32)
        ringA = sb("ringA", [128, 3, 2 * 8 * 128], BF16)
        ringB = sb("ringB", [128, 8, D], BF16)
        wuq = ringA[:, 0, 0:1536].rearrange("p (c n) -> p c n", c=2)
        wuk_sb = ringA[:, 0, 1536:2048]
        wuv_sb = ringA[:, 1, 0:512].rearrange("p (h d) -> p h d", d=DN)
        xres = sb("xres", [128, 4, D], F32)
        xs = sb("xs", [128, 2, D], BF16)
        hT = sb("hT", [128, 8, 512], BF16)
        bb = sb("bb", [128, 28, 512], BF16)
        ff = sb("ff", [128, 8, UW], F32)
        uext = sb("uext", [128, 4, UW], F32)
        PT = sb("PT", [128, 5, 512], BF16)
        sq = sb("sq", [128, 2, 512], BF16)
        accP = sb("accP", [128, 2, 512], F32)
        ckv_out = sb("ckv_out", [128, 4, R], F32)
        kr_out = sb("kr_out", [128, 4, DR], F32)
        pst = ff[:, 2, 0:PW]
        spsb = ff[:, 3, 0:PW]
        stat = sb("stat", [128, 16], F32)
        itile = sb("itile", [128, 512], mybir.dt.int32)

        pb = [ps(f"pb{i}", [128, 512], F32) for i in range(7)]
        psT = ps("psT", [128, 8, 128], BF16)

        def PS(i):
            return f"ps:{i}"

        def mm(out, lhsT, rhs, start, stop, r, w, tp=None):
            if tp is None:
                fn = lambda e: e.matmul(out, lhsT=lhsT, rhs=rhs, start=start, stop=stop)
            else:
                fn = lambda e: e.matmul(out, lhsT=lhsT, rhs=rhs, start=start, stop=stop,
                                        tile_position=tp)
            return T.add("pe", fn, r, w)

        def tr(out, in_, ident, r, w):
            return T.add("pe", lambda e: e.transpose(out=out, in_=in_, identity=ident), r, w)

        def act(out, in_, func, r, w, scale=None, accum=None):
            kw = {}
            if scale is not None:
                kw["scale"] = scale
            if accum is not None:
                kw["accum_out"] = accum
            return T.add("act", lambda e: e.activation(out=out, in_=in_, func=func, **kw), r, w)

        def ts(out, in0, s1, s2, op0, op1, r, w):
            if op1 is None:
                fn = lambda e: e.tensor_scalar(out=out, in0=in0, scalar1=s1, scalar2=None, op0=op0)
            else:
                fn = lambda e: e.tensor_scalar(out=out, in0=in0, scalar1=s1, scalar2=s2,
                                               op0=op0, op1=op1)
            return T.add("dve", fn, r, w)

        def tt(out, in0, in1, op, r, w):
            return T.add("dve", lambda e: e.tensor_tensor(out=out, in0=in0, in1=in1, op=op), r, w)

        def stt(out, in0, scalar, in1, op0, op1, r, w):
            return T.add("dve", lambda e: e.scalar_tensor_tensor(
                out=out, in0=in0, scalar=scalar, in1=in1, op0=op0, op1=op1), r, w)

        def cp(out, in_, r, w):
            return T.add("dve", lambda e: e.tensor_copy(out=out, in_=in_), r, w)

        def memset(out, val, w):
            return T.add("dve", lambda e: e.memset(out, val), (), w)

        def recip(out, in_, r, w):
            return T.add("dve", lambda e: e.reciprocal(out=out, in_=in_), r, w)

        def dma(q, out, in_, r, w, key):
            return T.add(q, lambda e: e.dma_start(out=out, in_=in_), r, w, dkey=key)

        def rstd_chain(buf, src, scale, rsrc, rname):
            ts(buf, src, scale, EPS, ALU.mult, ALU.add, rsrc, [rname])
            if LNEXP:
                act(buf, buf, AF.Ln, [rname], [rname])
                act(buf, buf, AF.Exp, [rname], [rname], scale=-0.5)
            else:
                act(buf, buf, AF.Sqrt, [rname], [rname])
                recip(buf, buf, [rname], [rname])

        T.bulk.update(["cast", "wres", "cres"])
        dma("pool", s_gu[:, :], w_gu_l[:, :], (), ["s_gu"], "cast")
        dma("pool", s_down[:, :], w_down_l[:, :], (), ["s_down"], "cast")
        dma("pool", s_wo[:, :], w_o_l[:, :], (), ["s_wo"], "cast")
        dma("pool", win[:].rearrange("p c n -> p (c n)"), w_in_l[:, :], (), ["win"], "wres")
        dma("pool", ringA[:, 0, 0:1536], w_uq_l[:, :], (), ["wuq"], "wres")
        dma("pool", wuk_sb, w_uk_l[:, :], (), ["wuk_sb"], "wres")
        dma("pool", ringA[:, 1, 0:512], w_uv_l[:, :], (), ["wuv_sb"], "wres")
        dma("pool", wpool[:].rearrange("p g d -> p (g d)"), w_pool_l[:, :], (), ["wpool"], "wres")
        for (t_, src, nm) in ((g1, g1_l, "g1"), (g2, g2_l, "g2"), (gq, gq_l, "gq"),
                              (gkv, gkv_l, "gkv"), (pscale, pscale_l, "pscale"),
                              (gop, gop_l, "gop"), (goa, goa_l, "goa"),
                              (ident_f, c_ident, "ident_f"), (iota, c_iota, "iota"),
                              (freq, c_freq, "freq"), (maskc, c_mask, "maskc")):
            dma("sp", t_[:], src[:, :], (), [nm], "cres")
        dma("sp", invc[:].rearrange("p g t -> p (g t)"), c_invc[:, :], (), ["invc"], "cres")
        dma("sp", gf_bc[:], g_final.partition_broadcast(128), (), ["gf_bc"], "cres")

        memset(ones_b[:], 1.0, ["ones_b"])
        memset(ones_f[:], 1.0, ["ones_f"])
        cp(ident_b[:], ident_f[:], ["ident_f"], ["ident_b"])
        tt(pg[:], pscale[:], gop[:], ALU.mult, ["pscale", "gop"], ["pg"])
        for rep in range(4):
            cp(wkr[:, :, rep * 32:(rep + 1) * 32], win[:, :, 384:416], ["win"], ["wkr"])
            ts(wkrB[:, :, rep * 32:rep * 32 + 16], win[:, :, 400:416], -1.0, None, ALU.mult, None,
               ["win"], ["wkrB"])
            cp(wkrB[:, :, rep * 32 + 16:rep * 32 + 32], win[:, :, 384:400], ["win"], ["wkrB"])
        for c in range(2):
            wq4 = wuq[:, c, :].rearrange("p (h e) -> p h e", e=96)
            cp(wuqN[:, c, :].rearrange("p (h e) -> p h e", e=64), wq4[:, :, 0:64], ["wuq"], ["wuqN"])
            cp(wuqA[:, c, :].rearrange("p (h e) -> p h e", e=32), wq4[:, :, 64:96], ["wuq"], ["wuqA"])
            wB = wuqB[:, c, :].rearrange("p (h e) -> p h e", e=32)
            ts(wB[:, :, 0:16], wq4[:, :, 80:96], -1.0, None, ALU.mult, None, ["wuq"], ["wuqB"])
            cp(wB[:, :, 16:32], wq4[:, :, 64:80], ["wuq"], ["wuqB"])
        for m in range(4):
            tr(psT[:, m, :], wuk_sb[:, m * 128:(m + 1) * 128], ident_b[:], ["wuk_sb", "ident_b"], ["psT"])
        cp(wukT[:], psT[:, 0:4, :], ["psT"], ["wukT"])
        memset(wuvpad[:], 0.0, ["wuvpad"])
        for half in range(2):
            cp(wuvpad[:].rearrange("p (m t) c -> p m t c", t=2)[:, :, half, half * 64:(half + 1) * 64],
               wuv_sb.rearrange("p (m t) d -> p m t d", t=2)[:, :, half, :],
               ["wuv_sb", "wuvpad"], ["wuvpad"])

        store_ops = []

        def sumsq_tiles(tiles, TP, statcol, src=None):
            lo, hi = statcol + tiles[0], statcol + tiles[-1] + 1
            cols = stat[:TP, lo:hi]
            sr = f"stat:{lo}"
            junk = sq[:].rearrange("p a b -> p (a b)")[:TP, :]
            if USE_ACCUM:
                memset(cols, 0.0, [sr])
                for i in tiles:
                    xin, xr = (src[i] if (src and i in src) else (xres[:TP, i, :], [f"xres:{i}"]))
                    act(junk, xin, AF.Square, xr + [sr], ["sq:0", "sq:1", sr],
                        accum=stat[:TP, statcol + i:statcol + i + 1])
            else:
                for i in tiles:
                    jf = ff[:, 6:8, :].rearrange("p a b -> p (a b)")[:TP, 0:D]
                    act(jf, xres[:TP, i, :], AF.Square, [f"xres:{i}"], ["ff:6", "ff:7"])
                    T.add("dve", lambda e, o=stat[:TP, statcol + i:statcol + i + 1], j=jf:
                          e.reduce_sum(out=o, in_=j, axis=mybir.AxisListType.X), ["ff:6", "ff:7"], [sr])
            rstd_chain(cols, cols, 1.0 / D, [sr], sr)
            return sr

        def norm_transpose(NT, TP, TB, gvec, gname, statcol, tiles=None, src=None):
            if tiles is None:
                tiles = list(range(NT))
            if not tiles:
                return
            sr = sumsq_tiles(tiles, TP, statcol, src)
            for i in tiles:
                sc = stat[:TP, statcol + i:statcol + i + 1]
                slot = i % 2
                xin, xr = (src[i] if (src and i in src) else (xres[:TP, i, :], [f"xres:{i}"]))
                act(xs[:TP, slot, :], xin, AF.Copy, xr + [sr], [f"xs:{slot}"], scale=sc)
                for c in range(8):
                    tr(psT[:, c, :TP], xs[:TP, slot, c * 128:(c + 1) * 128], ident_b[:TP, :TP],
                       [f"xs:{slot}", "ident_b"], ["psT"])
                for c in range(8):
                    ts(hT[:, c, i * TP:(i + 1) * TP], psT[:, c, :TP], gvec[:, c:c + 1], None, ALU.mult, None,
                       ["psT", gname], [f"hT:{c}"])

        ring_state = {"A": 0, "B": 0}
        rope_state = {}
        stage_pending = {}

        def proj_tokmajor(NT, TP, in_chunks, in_names, w_scr, wname, epilogue, after_pass=None, mid_last=None):
            nf = len(in_chunks)
            for p0 in range(0, NT, 2):
                tiles = list(range(p0, min(p0 + 2, NT)))
                for f in range(nf):
                    s = ring_state["B"] % 8
                    ring_state["B"] += 1
                    dma("sp", ringB[:, s, :], w_scr[f * 128:(f + 1) * 128, :], [wname], [f"rB:{s}"], f"rB:{s}")
                    for ti, i in enumerate(tiles):
                        for n in range(2):
                            bk = ti * 2 + n
                            mm(pb[bk][:TP, :], in_chunks[f][:, i * TP:(i + 1) * TP],
                               ringB[:, s, n * 512:(n + 1) * 512], f == 0, f == nf - 1,
                               [in_names[f], f"rB:{s}"], [PS(bk)])
                for ti, i in enumerate(tiles):
                    for n in range(2):
                        epilogue(i, n, pb[ti * 2 + n], PS(ti * 2 + n))
                if mid_last is not None and p0 + 2 >= NT:
                    mid_last()
                if after_pass is not None:
                    after_pass(tiles)

        def block(x_dram, y_dram, ockv, okr, opool, NT, TP, pos0, kcol0, nk_prev_tiles, causal_blk,
                  first_prompt, write_pool, load_tiles, next_x, pre_done=(), early_next=False, next_pos0=None):
            TB = NT * TP
            kt0 = kcol0 // 128
            kb = kcol0 // 512
            Kw = [f"Klat:{kb}", f"Krope:{kb}", f"V:{kb}"]
            for i in load_tiles:
                dma("pool", xres[:TP, i, :], x_dram[i * TP:(i + 1) * TP, :], (), [f"xres:{i}"], f"ldx:{i}")
            def emit_rope(p0, tb):
                Ct_, St_, tmp = ff[:, 0, :tb], ff[:, 1, :tb], ff[:, 2, :tb]
                ts(tmp, iota[:, :tb], float(p0), None, ALU.add, None, ["iota"], ["ff:2"])
                ts(tmp, tmp, freq[:, 0:1], None, ALU.mult, None, ["ff:2", "freq"], ["ff:2"])

                def sin_table(dst, dname, shift):
                    if shift != 0.0:
                        ts(dst, tmp, shift, None, ALU.add, None, ["ff:2"], [dname])
                        src, sname = dst, dname
                    else:
                        src, sname = tmp, "ff:2"
                    ts(itile[:, :tb], src, 1.0 / (2 * PI), None, ALU.mult, None, [sname], ["itile"])
                    stt(dst, itile[:, :tb], -2 * PI, src, ALU.mult, ALU.add, ["itile", sname], [dname])
                    ts(dst, dst, -PI, PI, ALU.max, ALU.min, [dname], [dname])
                    act(dst, dst, AF.Sin, [dname], [dname])
                sin_table(St_, "ff:1", 0.0)
                sin_table(Ct_, "ff:0", 0.5 * PI)
                rope_state["pos"] = p0
            Ct, St = ff[:, 0, :TB], ff[:, 1, :TB]
            if rope_state.get("pos") != pos0:
                emit_rope(pos0, TB)
            if STAGE < 2:
                return
            norm_transpose(NT, TP, TB, g1, "g1", 0, [i for i in range(NT) if i not in pre_done])
            hr = [f"hT:{c}" for c in range(8)]
            if STAGE < 3:
                return
            if SUB < 19:
                return
            cqT = [bb[:, 18, :TB], bb[:, 19, :TB]]
            rq = ff[:, 3, :TB]
            for m in range(2):
                for c in range(8):
                    mm(pb[m][:, :TB], win[:, c, m * 128:(m + 1) * 128], hT[:, c, :TB], c == 0, c == 7,
                       ["win", hr[c]], [PS(m)])
                ts(cqT[m], pb[m][:, :TB], gq[:, m:m + 1], None, ALU.mult, None, [PS(m), "gq"], [f"bb:{18 + m}"])
                if VV >= 12:
                    act(sq[:, m, :TB], pb[m][:, :TB], AF.Square, [PS(m)], [f"sq:{m}"])
            for c in range(8):
                mm(pb[2][:, :TB], win[:, c, 256:384], hT[:, c, :TB], c == 0, c == 7, ["win", hr[c]], [PS(2)])
            for c in range(8):
                mm(pb[3][:, :TB], wkr[:, c, :], hT[:, c, :TB], c == 0, c == 7, ["wkr", hr[c]], [PS(3)])
            for m in range(2):
                if VV >= 13:
                    mm(pb[4][:, :TB], ones_b[:], sq[:, m, :TB], m == 0, m == 1, ["ones_b", f"sq:{m}"], [PS(4)])
            if VV >= 14:
                rstd_chain(rq, pb[4][:, :TB], 1.0 / QL, [PS(4)], "ff:3")
            if SUB < 20:
                return
            if 2 in stage_pending:
                stage_pending.pop(2)()
            ckvT = ff[:, 4, :TB]
            rkv = ff[:, 5, :TB]
            act(sq[:, 0, :TB], pb[2][:, :TB], AF.Square, [PS(2)], ["sq:0"])
            mm(pb[5][:, :TB], ones_b[:], sq[:, 0, :TB], True, True, ["ones_b", "sq:0"], [PS(5)])
            rstd_chain(rkv, pb[5][:, :TB], 1.0 / R, [PS(5)], "ff:5")
            stt(ckvT, pb[2][:, :TB], gkv[:, 0:1], rkv, ALU.mult, ALU.mult, [PS(2), "gkv", "ff:5"], ["ff:4"])
            act(KlatT[:, kcol0:kcol0 + TB], ckvT, AF.Copy, ["ff:4"], [Kw[0]])
            for i in range(NT):
                tr(pb[6][:TP, i * 128:(i + 1) * 128], ckvT[:, i * TP:(i + 1) * TP], ident_f[:], ["ff:4", "ident_f"],
                   [PS(6)])
            cp(ckv_out[:TP, :NT, :], pb[6][:TP, :NT * 128].rearrange("p (i d) -> p i d", d=128), [PS(6)],
               ["ckv_out"])
            act(Vt[:TP, kt0:kt0 + NT, :], pb[6][:TP, :NT * 128].rearrange("p (i d) -> p i d", d=128), AF.Copy,
                [PS(6)], [Kw[2]])
            store_ops.append(dma("pool", ockv.rearrange("(i p) d -> p i d", p=TP), ckv_out[:TP, :NT, :],
                                 ["ckv_out"], (), "st_ckv"))
            if SUB < 21:
                return
            krT = ff[:, 4, :TB]
            t2 = ff[:, 5, :TB]
            for c in range(8):
                mm(pb[1][:, :TB], wkrB[:, c, :], hT[:, c, :TB], c == 0, c == 7, ["wkrB", hr[c]], [PS(1)])
            tt(krT, pb[3][:, :TB], Ct, ALU.mult, [PS(3), "ff:0"], ["ff:4"])
            tt(t2, pb[1][:, :TB], St, ALU.mult, [PS(1), "ff:1"], ["ff:5"])
            tt(krT, krT, t2, ALU.add, ["ff:4", "ff:5"], ["ff:4"])
            act(KropeT[:, kcol0:kcol0 + TB], krT, AF.Copy, ["ff:4"], [Kw[1]])
            for i in range(NT):
                tr(pb[6][:TP, i * 128:(i + 1) * 128], krT[:, i * TP:(i + 1) * TP], ident_f[:], ["ff:4", "ident_f"],
                   [PS(6)])
            cp(kr_out[:TP, :NT, :], pb[6][:TP, :NT * 128].rearrange("p (i d) -> p i d", d=128)[:, :, 0:DR],
               [PS(6)], ["kr_out"])
            store_ops.append(dma("pool", okr.rearrange("(i p) d -> p i d", p=TP), kr_out[:TP, :NT, :],
                                 ["kr_out"], (), "st_kr"))
            if SUB < 22:
                return
            for g in range(4):
                bk = 2 + (g % 2)
                for c in range(8):
                    mm(pb[bk][:, :TB], win[:, c, 416 + g * 128:416 + (g + 1) * 128], hT[:, c, :TB], c == 0, c == 7,
                       ["win", hr[c]], [PS(bk)])
                act(uext[:, g, EXT:EXT + TB], pb[bk][:, :TB], AF.Copy, [PS(bk)], [f"uext:{g}"])
            if STAGE < 4:
                return
            if 3 in stage_pending:
                stage_pending.pop(3)()
            opT = [bb[:, 10 + g, :TB] for g in range(4)]
            for g, wd in enumerate(WINDOWS):
                ur = f"uext:{g}"
                W = EXT + TB
                cur, curr = uext[:, g, :], ur
                bufs = [(ff[:, 6, :], "ff:6"), (ff[:, 7, :], "ff:7")]
                k, bi = 1, 0
                while k < wd:
                    lo = EXT - (wd - 2 * k)
                    dst, dr = bufs[bi]
                    if POOL_MIX:
                        T.add("pool", lambda e, o=dst[:, lo:W], a_=cur[:, lo:W], b_=cur[:, lo - k:W - k]:
                              e.tensor_tensor(out=o, in0=a_, in1=b_, op=ALU.add), [curr], [dr])
                    else:
                        tt(dst[:, lo:W], cur[:, lo:W], cur[:, lo - k:W - k], ALU.add, [curr], [dr])
                    cur, curr = dst, dr
                    bi ^= 1
                    k *= 2
                pooled = bb[:, g, :TB]
                stt(pooled, cur[:, EXT:W], 1.0 / wd, uext[:, g, EXT:W], ALU.mult, ALU.subtract, [curr, ur],
                    [f"bb:{g}"])
                if first_prompt:
                    t16 = ff[:, 2, 0:16]
                    tt(t16, cur[:, EXT:EXT + 16], invc[:, g, :], ALU.mult, [curr, "invc"], ["ff:2"])
                    tt(pooled[:, 0:16], t16, uext[:, g, EXT:EXT + 16], ALU.subtract, ["ff:2", ur], [f"bb:{g}"])
                bk = g % 2
                mm(pb[bk][:, :TB], wpool[:, g, :], pooled, True, True, ["wpool", f"bb:{g}"], [PS(bk)])
                ts(opT[g], pb[bk][:, :TB], pg[:, g:g + 1], None, ALU.mult, None, [PS(bk), "pg"], [f"bb:{10 + g}"])
                act(sq[:, g % 2, :TB], pb[bk][:, :TB], AF.Square, [PS(bk), "pscale"], [f"sq:{g % 2}"],
                    scale=pscale[:, g:g + 1])
                mm(pb[4][:, :TB], ones_b[:], sq[:, g % 2, :TB], g == 0, g == 3, ["ones_b", f"sq:{g % 2}"], [PS(4)])
            rp = ff[:, 5, :TB]
            rstd_chain(rp, pb[4][:, :TB], 1.0 / PW, [PS(4)], "ff:5")
            for g in range(4):
                tt(opT[g], opT[g], rp, ALU.mult, [f"bb:{10 + g}", "ff:5"], [f"bb:{10 + g}"])
            if write_pool:
                for g in range(4):
                    tr(pb[6][:15, g * 128:(g + 1) * 128], uext[:, g, EXT + TB - 15:EXT + TB], ident_f[:],
                       [f"uext:{g}", "ident_f"], [PS(6)])
                cp(pst[:15, :], pb[6][:15, :], [PS(6)], ["ff:2"])
                store_ops.append(dma("pool", opool[:, :], pst[:15, :], ["ff:2"], (), "st_pool"))
            else:
                cp(uext[:, :, 1:16], uext[:, :, TB + 1:TB + 16], [f"uext:{g}" for g in range(4)],
                   [f"uext:{g}" for g in range(4)])
            if STAGE < 5:
                return
            qn = [bb[:, 14 + m, :TB] for m in range(4)]
            for m in range(4):
                bk = m % 2
                for c in range(2):
                    mm(pb[bk][:, :TB], wuqN[:, c, m * 128:(m + 1) * 128], cqT[c], c == 0, c == 1,
                       ["wuqN", f"bb:{18 + c}"], [PS(bk)])
                tt(qn[m], pb[bk][:, :TB], rq, ALU.mult, [PS(bk), "ff:3"], [f"bb:{14 + m}"])
            batch_heads = not causal_blk
            if batch_heads:
                QlatT = [bb[:, 0, h * TB:(h + 1) * TB] for h in range(NH)]
                QLR = ["bb:0"] * NH
            else:
                QlatT = [bb[:, h, :TB] for h in range(NH)]
                QLR = [f"bb:{h}" for h in range(NH)]
            for h in range(NH):
                m, half = h // 2, h % 2
                bk = 2 + (h % 2)
                mm(pb[bk][:, :TB], wukT[half * 64:(half + 1) * 64, m, :], qn[m][half * 64:(half + 1) * 64, :],
                   True, True, ["wukT", f"bb:{14 + m}"], [PS(bk)])
                act(QlatT[h], pb[bk][:, :TB], AF.Copy, [PS(bk)], [QLR[h]])
            QRC = [8, 9, 20, 21, 22, 23, 24, 25]
            if batch_heads:
                QropeM = [bb[:, 8, h * TB:(h + 1) * TB] for h in range(NH)]
                QRR = ["bb:8"] * NH
            else:
                QropeM = [bb[:, QRC[h], :TB] for h in range(NH)]
                QRR = [f"bb:{QRC[h]}" for h in range(NH)]
            for a in range(2):
                for c in range(2):
                    mm(pb[0][:, :TB], wuqA[:, c, a * 128:(a + 1) * 128], cqT[c], c == 0, c == 1,
                       ["wuqA", f"bb:{18 + c}"], [PS(0)])
                for c in range(2):
                    mm(pb[1][:, :TB], wuqB[:, c, a * 128:(a + 1) * 128], cqT[c], c == 0, c == 1,
                       ["wuqB", f"bb:{18 + c}"], [PS(1)])
                t1, t2 = ff[:, 6, :TB], ff[:, 7, :TB]
                tt(t1, pb[0][:, :TB], Ct, ALU.mult, [PS(0), "ff:0"], ["ff:6"])
                tt(t2, pb[1][:, :TB], St, ALU.mult, [PS(1), "ff:1"], ["ff:7"])
                tt(t1, t1, t2, ALU.add, ["ff:6", "ff:7"], ["ff:6"])
                tt(t1, t1, rq, ALU.mult, ["ff:6", "ff:3"], ["ff:6"])
                for g in range(4):
                    ts(QropeM[4 * a + g], t1, maskc[:, g:g + 1], None, ALU.mult, None, ["ff:6", "maskc"],
                       [QRR[4 * a + g]])
            if STAGE < 6:
                return
            ktiles = []
            for j in range(nk_prev_tiles):
                ktiles.append((j * 128, j, 128, 0, False, j // 4))
            if causal_blk:
                for m in range(NT):
                    ktiles.append((kcol0 + m * 128, kt0 + m, 128, m * 128, True, kb))
            else:
                ktiles.append((kcol0, kt0, TB, 0, False, kb))
            oaT = [bb[:, 14 + m, :TB] for m in range(4)]
            LA = 2
            SB_ = (0, 1, 2)
            NPT = 5
            nk = len(ktiles)
            vheads = [list(range(NH))] if batch_heads else [[h] for h in range(NH)]
            W = TB * len(vheads[0])
            if batch_heads:
                QLv = [bb[:, 0, :W]]
                QRv = [bb[:, 8, :W]]
                QLn, QRn = [["bb:0"]], [["bb:8"]]
            else:
                QLv, QRv = QlatT, QropeM
                QLn, QRn = [[x] for x in QLR], [[x] for x in QRR]
            jobs = [(v, ki) for v in range(len(vheads)) for ki in range(nk)]
            pend = []

            def do_pv(v, ki, psl):
                kc, vt, kr_, n0, diag, kblk = ktiles[ki]
                ob = 3 + (v % 2)
                mm(pb[ob][:, n0:W], Vt[:kr_, vt, :], PT[:kr_, psl, n0:W], ki == 0, ki == nk - 1,
                   [f"V:{kblk}", f"PT:{psl}"], [PS(ob)])

            def epi1(v):
                ob = 3 + (v % 2)
                acc = ff[:, 6 + (v % 2), :W]
                accr = f"ff:{6 + (v % 2)}"
                mm(pb[5][:, :W], ones_f[:], acc, True, False, ["ones_f", accr], [PS(5)])
                mm(pb[5][:, :W], ones_f[:], accP[:, v % 2, :W], False, True, ["ones_f", f"accP:{v % 2}"], [PS(5)])
                rden = ff[:, 4 + (v % 2), :W]
                rdr = f"ff:{4 + (v % 2)}"
                act(rden, pb[5][:, :W], AF.Ln, [PS(5)], [rdr])
                act(rden, rden, AF.Exp, [rdr], [rdr], scale=-1.0)
                olat = sq[:, v % 2, :W]
                tt(olat, pb[ob][:, :W], rden, ALU.mult, [PS(ob), rdr], [f"sq:{v % 2}"])

            def epi2(v):
                for j, h in enumerate(vheads[v]):
                    m = h // 2
                    olat = sq[:, v % 2, j * TB:(j + 1) * TB]
                    mm(pb[6][:, :TB], wuvpad[:, h, :], olat, h % 2 == 0, h % 2 == 1, ["wuvpad", f"sq:{v % 2}"],
                       [PS(6)])
                    if h % 2 == 1:
                        act(oaT[m], pb[6][:, :TB], AF.Copy, [PS(6)], [f"bb:{14 + m}"])

            def flush(upto):
                keep = []
                for (at, fn) in pend:
                    if at <= upto:
                        fn()
                    else:
                        keep.append((at, fn))
                pend[:] = keep

            for idx, (v, ki) in enumerate(jobs):
                kc, vt, kr_, n0, diag, kblk = ktiles[ki]
                sb_ = SB_[idx % len(SB_)]
                psl = idx % NPT
                acc = ff[:, 6 + (v % 2), :W]
                accr = f"ff:{6 + (v % 2)}"
                mm(pb[sb_][:kr_, n0:W], KlatT[:, kc:kc + kr_], QLv[v][:, n0:W], True, False,
                   [f"Klat:{kblk}"] + QLn[v], [PS(sb_)])
                mm(pb[sb_][:kr_, n0:W], KropeT[:, kc:kc + kr_], QRv[v][:, n0:W], False, True,
                   [f"Krope:{kblk}"] + QRn[v], [PS(sb_)])
                act(PT[:kr_, psl, n0:W], pb[sb_][:kr_, n0:W], AF.Exp, [PS(sb_)], [f"PT:{psl}"],
                    scale=SM_SCALE)
                if diag:
                    memset(PT[64:128, psl, n0:n0 + 64], 0.0, [f"PT:{psl}"])
                if ki == 0:
                    cp(acc, PT[:, psl, :W], [f"PT:{psl}"], [accr])
                    memset(accP[:, v % 2, :W], 0.0, [f"accP:{v % 2}"])
                elif ki % 2 == 1:
                    a2 = accP[:, v % 2, :W]
                    tt(a2[:kr_, n0:W], a2[:kr_, n0:W], PT[:kr_, psl, n0:W], ALU.add,
                       [f"accP:{v % 2}", f"PT:{psl}"], [f"accP:{v % 2}"])
                else:
                    tt(acc[:kr_, n0:W], acc[:kr_, n0:W], PT[:kr_, psl, n0:W], ALU.add,
                       [accr, f"PT:{psl}"], [accr])
                pend.append((idx + LA, lambda v=v, ki=ki, psl=psl: do_pv(v, ki, psl)))
                if ki == nk - 1:
                    pend.append((idx + LA + 1, lambda v=v: epi1(v)))
                    pend.append((idx + LA + 4, lambda v=v: epi2(v)))
                flush(idx)
            flush(10 ** 9)
            for m in range(4):
                act(PT[:, m, :TB], oaT[m], AF.Square, [f"bb:{14 + m}"], [f"PT:{m}"])
            for m in range(4):
                mm(pb[2][:, :TB], ones_b[:], PT[:, m, :TB], m == 0, m == 3, ["ones_b", f"PT:{m}"], [PS(2)])
            ra = ff[:, 4, :TB]
            rstd_chain(ra, pb[2][:, :TB], 1.0 / PW, [PS(2)], "ff:4")
            for m in range(4):
                stt(oaT[m], oaT[m], goa[:, m:m + 1], ra, ALU.mult, ALU.mult, [f"bb:{14 + m}", "goa", "ff:4"],
                    [f"bb:{14 + m}"])
            if STAGE < 7:
                return
            def epi_res(i, n, pst_, psr):
                tt(xres[:TP, i, n * 512:(n + 1) * 512], pst_[:TP, :], xres[:TP, i, n * 512:(n + 1) * 512], ALU.add,
                   [psr, f"xres:{i}"], [f"xres:{i}"])
            if NT == 4 and SPLIT_NORM2:
                proj_tokmajor(NT, TP, oaT + opT,
                              [f"bb:{14 + m}" for m in range(4)] + [f"bb:{10 + g}" for g in range(4)],
                              s_wo, "s_wo", epi_res, None,
                              lambda: norm_transpose(NT, TP, TB, g2, "g2", 4, [0, 1]))
                n2_tiles = [2, 3]
            else:
                proj_tokmajor(NT, TP, oaT + opT,
                              [f"bb:{14 + m}" for m in range(4)] + [f"bb:{10 + g}" for g in range(4)],
                              s_wo, "s_wo", epi_res)
                n2_tiles = None
            if STAGE < 8:
                return
            norm_transpose(NT, TP, TB, g2, "g2", 4, n2_tiles)
            if STAGE < 9:
                return
            actT = [bb[:, f, :TB] for f in range(NF)]
            use_stage = bool(STAGE_X and early_next and next_x is not None and NT == 4)
            stgA = ff[:, 4:6, :].rearrange("p a b -> p (a b)")[:, 0:D]
            stgB = ff[:, 6:8, :].rearrange("p a b -> p (a b)")[:, 0:D]
            stg_src = {2: (stgA, ["ff:4", "ff:5"]), 3: (stgB, ["ff:6", "ff:7"])}
            if EARLY_ROPE and next_pos0 is not None:
                emit_rope(next_pos0, 512)
            if use_stage:
                dma("pool", stgA, next_x[256:384, :], (), ["ff:4", "ff:5"], "ldxs:2")
            for f in range(NF):
                s = ring_state["A"] % 3
                ring_state["A"] += 1
                dma("sp", ringA[:, s, :], s_gu[f * 128:(f + 1) * 128, :], ["s_gu"],
                    [f"rA:{s}"] + {0: ["wuq", "wuk_sb"], 1: ["wuv_sb"]}.get(s, []), f"rA:{s}")
                wv = ringA[:, s, :].rearrange("p (t c n) -> p t c n", t=2, c=8)
                bg, bu = (f % 2) * 2, (f % 2) * 2 + 1
                for c in range(8):
                    mm(pb[bg][:, :TB], wv[:, 0, c, :], hT[:, c, :TB], c == 0, c == 7, [f"rA:{s}", hr[c]], [PS(bg)])
                for c in range(8):
                    mm(pb[bu][:, :TB], wv[:, 1, c, :], hT[:, c, :TB], c == 0, c == 7, [f"rA:{s}", hr[c]], [PS(bu)])
                sg = ff[:, 6 + (f % 2), :TB]
                sgr = f"ff:{6 + (f % 2)}"
                act(sg, pb[bg][:, :TB], AF.Silu, [PS(bg)], [sgr])
                tt(actT[f], sg, pb[bu][:, :TB], ALU.mult, [sgr, PS(bu)], [f"bb:{f}"])
            if use_stage:
                dma("pool", stgB, next_x[384:512, :], (), ["ff:6", "ff:7"], "ldxs:3")

            def final_tiles(tiles):
                sr = sumsq_tiles(tiles, TP, 8)
                for i in tiles:
                    sc = stat[:TP, 8 + i:9 + i]
                    stt(xres[:TP, i, :], xres[:TP, i, :], sc, gf_bc[:TP, :], ALU.mult, ALU.mult,
                        [f"xres:{i}", sr, "gf_bc"], [f"xres:{i}"])
                    store_ops.append(dma("pool", y_dram[i * TP:(i + 1) * TP, :], xres[:TP, i, :], [f"xres:{i}"], (),
                                         f"sty:{i}"))
                    if next_x is not None:
                        if use_stage and i >= 2:
                            stage_pending[i] = (lambda i=i: act(xres[:, i, :], stg_src[i][0], AF.Copy,
                                                                stg_src[i][1], [f"xres:{i}"]))
                        else:
                            dma("pool", xres[:, i, :], next_x[i * 128:(i + 1) * 128, :], (), [f"xres:{i}"],
                                f"ldx:{i}")
            if STAGE < 10:
                return
            def early_norm():
                if use_stage:
                    norm_transpose(4, 128, 512, g1, "g1", 0, [0, 1, 2, 3], stg_src)
                else:
                    norm_transpose(4, 128, 512, g1, "g1", 0, [0, 1])
            proj_tokmajor(NT, TP, actT, [f"bb:{f}" for f in range(NF)], s_down, "s_down", epi_res, final_tiles,
                          early_norm if (early_next and next_x is not None and NT == 4) else None)

        if DO_SAMPLE and STAGE >= 1:
            kres = [f"Klat:{b}" for b in range(8)]
            rres = [f"Krope:{b}" for b in range(8)]
            vres = [f"V:{b}" for b in range(8)]
            stg = ff[:, 6:8, :].rearrange("p a b -> p (a b)")[:, 0:1024]
            for q in range(4):
                dma("sp", stg.rearrange("p (t r) -> p t r", r=128),
                    cc_kv[q * 1024:(q + 1) * 1024, :].rearrange("(t p) r -> p t r", p=128), (), ["ff:6", "ff:7"],
                    "ldc")
                act(Vt[:, q * 8:(q + 1) * 8, :], stg.rearrange("p (t r) -> p t r", r=128), AF.Copy,
                    ["ff:6", "ff:7"], [vres[2 * q], vres[2 * q + 1]])
            krst = bb[:, 0:8, :].rearrange("p a (t q) -> p (a t) q", q=128)
            stg2 = ff[:, 4:6, :].rearrange("p a b -> p (a b)")[:, 0:1024].rearrange("p (t d) -> p t d", d=32)
            for q in range(4):
                dma("sp", stg2[:, q * 8:(q + 1) * 8, :],
                    cc_kr[q * 1024:(q + 1) * 1024, :].rearrange("(t p) d -> p t d", p=128), (), ["ff:4", "ff:5"],
                    f"ldk:{q}")
            for rep in range(4):
                cp(krst[:, :, rep * 32:(rep + 1) * 32], stg2, ["ff:4", "ff:5"], [f"bb:{i}" for i in range(8)])
            dma("sp", spsb[:15, :], st_pool[:, :], (), ["ff:3"], "ldsp")
            for t8 in range(4 if SUB >= 1 else 0):
                for j in range(8):
                    t = t8 * 8 + j
                    tr(psT[:, j, :], Vt[:, t, :], ident_b[:], [f"V:{t // 4}", "ident_b"], ["psT"])
                cp(KlatT[:, t8 * 1024:(t8 + 1) * 1024], psT[:].rearrange("p j k -> p (j k)"), ["psT"],
                   [kres[2 * t8], kres[2 * t8 + 1]])
            for t8 in range(4 if SUB >= 2 else 0):
                for j in range(8):
                    t = t8 * 8 + j
                    tr(psT[:, j, :], krst[:, t, :], ident_b[:], [f"bb:{t // 4}", "ident_b"], ["psT"])
                cp(KropeT[:, t8 * 1024:(t8 + 1) * 1024], psT[:].rearrange("p j k -> p (j k)"), ["psT"],
                   [rres[2 * t8], rres[2 * t8 + 1]])
            for g in range(4 if SUB >= 3 else 0):
                tr(pb[6][:, g * 128:g * 128 + 15], spsb[:15, g * 128:(g + 1) * 128], ident_f[:15, :15],
                   ["ff:3", "ident_f"], [PS(6)])
            cp(uext[:, :, 1:16], pb[6][:, :].rearrange("p (g k) -> p g k", k=128)[:, :, 0:15], [PS(6)],
               [f"uext:{g}" for g in range(4)])
            for i in range(1, 4):
                dma("pool", xres[:, i, :], xp[i * 128:(i + 1) * 128, :], (), [f"xres:{i}"], f"ldx:{i}")
            block(xsm, y_s, o_ckv_s, o_kr_s, o_pool_s, 1, DEC, PAST, PAST, 32, False, False, True, [0],
                  xp[0:512, :] if NBLK > 0 else None, next_pos0=(0 if NBLK > 0 else None))
            first_loads = []
        else:
            first_loads = [0, 1, 2, 3]

        memset(uext[:, :, 0:16], 0.0, [f"uext:{g}" for g in range(4)])
        for J in range(NBLK):
            block(xp[J * 512:(J + 1) * 512, :], y_p[J * 512:(J + 1) * 512, :],
                  o_ckv_p[J * 512:(J + 1) * 512, :], o_kr_p[J * 512:(J + 1) * 512, :], o_pool_p,
                  4, 128, J * 512, J * 512, 4 * J, True, J == 0, J == SEQ // 512 - 1,
                  first_loads if J == 0 else [],
                  xp[(J + 1) * 512:(J + 2) * 512, :] if J + 1 < NBLK else None,
                  pre_done=(((0, 1, 2, 3) if STAGE_X else (0, 1)) if (J > 0 and EARLY_NORM) else ()),
                  early_next=bool(EARLY_NORM), next_pos0=((J + 1) * 512 if J + 1 < NBLK else None))

        lastst = {}
        for op in store_ops:
            lastst[op.dkey] = op
        T.add("sp", None, (), (), extra=list(lastst.values()))

        dkeys = T.resolve()
        esem = {e: es.enter_context(nc.semaphore(f"e_{e}")) for e in Tracker.ENGS}
        dsem = {k: es.enter_context(nc.semaphore("d_" + k.replace(":", "_"))) for k in dkeys}
        with nc.Block() as blk:
            @blk.tensor
            def _(e):
                T.emit("pe", e, esem, dsem)

            @blk.scalar
            def _(e):
                T.emit("act", e, esem, dsem)

            @blk.vector
            def _(e):
                T.emit("dve", e, esem, dsem)

            @blk.gpsimd
            def _(e):
                T.emit("pool", e, esem, dsem)

            @blk.sync
            def _(e):
                T.emit("sp", e, esem, dsem)
    return nc


def _chunked(v, n):
    return np.ascontiguousarray(np.asarray(v, np.float32).reshape(n, 128).T)


def kernel(x_prompt, x_sample, cache_kv_latent, cache_k_rope, state_pool,
           g_norm1, w_in, g_q, w_uq, g_kv, w_uk, w_uv, w_pool, pool_scale,
           g_out_attn, g_out_pool, w_o, g_norm2, w_gate, w_up, w_down, g_final):
    f = lambda a: np.ascontiguousarray(np.asarray(a, np.float32))
    w_in0, w_uq0 = f(w_in)[0], f(w_uq)[0]
    shared = {
        "w_in_l": f(w_in0.reshape(8, 128, INW).transpose(1, 0, 2).reshape(128, 8 * INW)),
        "w_uq_l": f(w_uq0.reshape(2, 128, 768).transpose(1, 0, 2).reshape(128, 2 * 768)),
        "w_uk_l": f(f(w_uk)[0].transpose(1, 0, 2).reshape(128, NH * DN)),
        "w_uv_l": f(f(w_uv)[0].transpose(1, 0, 2).reshape(128, NH * DN)),
        "w_pool_l": f(f(w_pool)[0].transpose(1, 0, 2).reshape(128, 4 * 128)),
        "w_down_l": f(f(w_down)[0]),
        "w_o_l": f(f(w_o)[0]),
        "g1_l": _chunked(f(g_norm1)[0], 8), "g2_l": _chunked(f(g_norm2)[0], 8),
        "gq_l": _chunked(f(g_q)[0], 2), "gkv_l": _chunked(f(g_kv)[0], 1),
        "pscale_l": _chunked(f(pool_scale)[0], 4), "gop_l": _chunked(f(g_out_pool)[0], 4),
        "goa_l": _chunked(f(g_out_attn)[0], 4),
        "g_final": f(g_final),
    }
    wg = f(w_gate)[0].reshape(8, 128, NF, 128).transpose(2, 1, 0, 3)
    wu = f(w_up)[0].reshape(8, 128, NF, 128).transpose(2, 1, 0, 3)
    shared["w_gu_l"] = f(np.stack([wg, wu], axis=2).reshape(NF * 128, 2 * 8 * 128))
    shared["c_ident"] = np.eye(128, dtype=np.float32)
    shared["c_iota"] = np.ascontiguousarray(np.broadcast_to(np.arange(512, dtype=np.float32), (128, 512)))
    fr = np.power(np.float32(10000.0), -np.arange(0, DR, 2, dtype=np.float32) / np.float32(DR)).astype(np.float32)
    shared["c_freq"] = np.ascontiguousarray(fr[np.arange(128) % 16].reshape(128, 1))
    invc = np.zeros((4, 16), np.float32)
    for g, wd in enumerate(WINDOWS):
        invc[g] = 1.0 / np.minimum(np.arange(16) + 1, wd).astype(np.float32)
    shared["c_invc"] = np.ascontiguousarray(np.broadcast_to(invc.reshape(1, 64), (128, 64)))
    shared["c_mask"] = np.ascontiguousarray((np.arange(128)[:, None] // 32 == np.arange(4)[None, :]).astype(np.float32))

    xp_, xs_ = f(x_prompt), f(x_sample)
    ckv_, ckr_, stp_ = f(cache_kv_latent)[0], f(cache_k_rope)[0], f(state_pool)[0]
    in_maps = []
    for c in range(NCORES):
        m = dict(shared)
        m.update({"xp": xp_[c], "xsm": xs_[c], "cc_kv": ckv_[c], "cc_kr": ckr_[c], "st_pool": stp_[c]})
        in_maps.append(m)
    nc = build_program()
    res = run_bass_kernel_spmd(nc, in_maps, core_ids=list(range(NCORES)))
    rs = res.results
    st = lambda k: np.stack([np.asarray(rs[c % NCORES][k], np.float32) for c in range(8)], axis=0)
    return (st("y_p"), st("y_s"), st("o_ckv_p")[None], st("o_kr_p")[None], st("o_pool_p")[None],
            st("o_ckv_s")[None], st("o_kr_s")[None], st("o_pool_s")[None])
```

```python
import os
import math
from contextlib import ExitStack
import numpy as np
import concourse.bass as bass
import concourse.mybir as mybir
from concourse.bass_utils import run_bass_kernel_spmd

F32 = mybir.dt.float32
BF16 = mybir.dt.bfloat16
AF = mybir.ActivationFunctionType
ALU = mybir.AluOpType

D = 1024
SEQ = 8192
DEC = 32
PAST = 4096
NH = 8
R = 128
DN = 64
DR = 32
QL = 256
PW = 512
DFF = 2816
NF = DFF // 128
INW = 928
EPS = 1e-6
SM_SCALE = 1.0 / math.sqrt(DN + DR)
WINDOWS = (2, 4, 8, 16)
PI = math.pi
EXT = 16
UW = EXT + 512

NBLK = 16
DO_SAMPLE = 1
STAGE = 99
NCORES = 8
SUB = 99
VV = 99
POOL_ACC = 0
USE_ACCUM = 1
LNEXP = 1
EARLY_NORM = 1
POOL_MIX = 1
SPLIT_NORM2 = 1
STAGE_X = 1
EARLY_ROPE = 1


class _Op:
    __slots__ = ("eng", "fn", "deps", "sig", "cnt", "dkey", "dcnt", "idx")


class Tracker:
    ENGS = ("pe", "act", "dve", "pool", "sp")

    def __init__(self):
        self.ops = []
        self.lw = {}
        self.rd = {}
        self.eng_ops = {e: [] for e in self.ENGS}
        self.bulk = set()
        self.last_dma = {}

    def add(self, eng, fn, r=(), w=(), dkey=None, extra=()):
        op = _Op()
        op.eng = eng
        op.fn = fn
        op.dkey = dkey
        op.sig = False
        op.cnt = 0
        op.dcnt = 0
        deps = {}

        def need(p):
            if p is None:
                return
            if p.dkey is None and dkey is None and p.eng == "pe" and eng == "pe":
                return
            k = p.dkey if p.dkey is not None else p.eng
            q = deps.get(k)
            if q is None or q.idx < p.idx:
                deps[k] = p

        for x in r:
            need(self.lw.get(x))
            if x.startswith("ps"):
                rr = self.rd.get(x)
                if rr:
                    for k_, p in rr.items():
                        if k_ != (dkey if dkey is not None else eng):
                            need(p)
        for x in w:
            need(self.lw.get(x))
            rr = self.rd.get(x)
            if rr:
                for p in rr.values():
                    need(p)
        for p in extra:
            need(p)
        op.deps = list(deps.values())
        op.idx = len(self.ops)
        self.ops.append(op)
        self.eng_ops[eng].append(op)
        sk = dkey if dkey is not None else eng
        for x in r:
            d = self.rd.get(x)
            if d is None:
                d = self.rd[x] = {}
            d[sk] = op
        for x in w:
            self.lw[x] = op
            self.rd[x] = {}
        if dkey is not None:
            self.last_dma[dkey] = op
        return op

    def resolve(self):
        for op in self.ops:
            for p in op.deps:
                if p.dkey is None:
                    p.sig = True
        for e in self.ENGS:
            c = 0
            for op in self.eng_ops[e]:
                if op.dkey is None and op.sig:
                    c += 1
                    op.cnt = c
        tot = {}
        for op in self.ops:
            if op.dkey is not None:
                tot[op.dkey] = tot.get(op.dkey, 0) + 16
                op.dcnt = tot[op.dkey]
        for op in self.ops:
            if op.dkey in self.bulk:
                op.dcnt = tot[op.dkey]
        return sorted(tot.keys())

    def emit(self, eng_name, e, esem, dsem):
        known = {}
        for op in self.eng_ops[eng_name]:
            for p in op.deps:
                if p.dkey is not None:
                    sem, val, k = dsem[p.dkey], p.dcnt, ("d", p.dkey)
                else:
                    sem, val, k = esem[p.eng], p.cnt, ("e", p.eng)
                if known.get(k, 0) < val:
                    e.wait_ge(sem, val)
                    known[k] = val
            if op.fn is None:
                continue
            ins = op.fn(e)
            if op.dkey is not None:
                ins.then_inc(dsem[op.dkey], 16)
            elif op.sig:
                ins.then_inc(esem[op.eng], 1)


def build_program():
    nc = bass.Bass("TRN2", target_bir_lowering=False)
    T = Tracker()

    def din(name, shape, dt=F32):
        return nc.dram_tensor(name, list(shape), dt, kind="ExternalInput").ap()

    def dout(name, shape):
        return nc.dram_tensor(name, list(shape), F32, kind="ExternalOutput").ap()

    def dscr(name, shape, dt=BF16):
        return nc.dram_tensor(name, list(shape), dt, kind="Internal").ap()

    xp = din("xp", [SEQ, D])
    xsm = din("xsm", [DEC, D])
    cc_kv = din("cc_kv", [PAST, R])
    cc_kr = din("cc_kr", [PAST, DR])
    st_pool = din("st_pool", [15, PW])
    w_in_l = din("w_in_l", [128, 8 * INW])
    w_uq_l = din("w_uq_l", [128, 2 * 768])
    w_uk_l = din("w_uk_l", [128, NH * DN])
    w_uv_l = din("w_uv_l", [128, NH * DN])
    w_pool_l = din("w_pool_l", [128, 4 * 128])
    w_gu_l = din("w_gu_l", [NF * 128, 2 * 8 * 128])
    w_down_l = din("w_down_l", [NF * 128, D])
    w_o_l = din("w_o_l", [8 * 128, D])
    g1_l = din("g1_l", [128, 8])
    g2_l = din("g2_l", [128, 8])
    gq_l = din("gq_l", [128, 2])
    gkv_l = din("gkv_l", [128, 1])
    pscale_l = din("pscale_l", [128, 4])
    gop_l = din("gop_l", [128, 4])
    goa_l = din("goa_l", [128, 4])
    g_final = din("g_final", [D])
    c_ident = din("c_ident", [128, 128])
    c_iota = din("c_iota", [128, 512])
    c_freq = din("c_freq", [128, 1])
    c_invc = din("c_invc", [128, 64])
    c_mask = din("c_mask", [128, 4])

    y_p = dout("y_p", [SEQ, D])
    y_s = dout("y_s", [DEC, D])
    o_ckv_p = dout("o_ckv_p", [SEQ, R])
    o_kr_p = dout("o_kr_p", [SEQ, DR])
    o_pool_p = dout("o_pool_p", [15, PW])
    o_ckv_s = dout("o_ckv_s", [DEC, R])
    o_kr_s = dout("o_kr_s", [DEC, DR])
    o_pool_s = dout("o_pool_s", [15, PW])

    s_gu = dscr("s_gu", [NF * 128, 2 * 8 * 128])
    s_down = dscr("s_down", [NF * 128, D])
    s_wo = dscr("s_wo", [8 * 128, D])

    es = ExitStack()
    with es:
        def sb(name, shape, dt):
            return es.enter_context(nc.sbuf_tensor(name, list(shape), dt))

        def ps(name, shape, dt):
            return es.enter_context(nc.psum_tensor(name, list(shape), dt))

        KlatT = sb("KlatT", [128, SEQ], BF16)
        KropeT = sb("KropeT", [128, SEQ], BF16)
        Vt = sb("Vt", [128, SEQ // 128, R], BF16)
        win = sb("win", [128, 8, INW], BF16)
        wkr = sb("wkr", [128, 8, 128], BF16)
        wkrB = sb("wkrB", [128, 8, 128], BF16)
        wuqN = sb("wuqN", [128, 2, 512], BF16)
        wuqA = sb("wuqA", [128, 2, 256], BF16)
        wuqB = sb("wuqB", [128, 2, 256], BF16)
        wukT = sb("wukT", [128, 4, 128], BF16)
        wuvpad = sb("wuvpad", [128, NH, 128], BF16)
        wpool = sb("wpool", [128, 4, 128], BF16)
        gf_bc = sb("gf_bc", [128, D], F32)
        g1 = sb("g1", [128, 8], F32)
        g2 = sb("g2", [128, 8], F32)
        gq = sb("gq", [128, 2], F32)
        gkv = sb("gkv", [128, 1], F32)
        pscale = sb("pscale", [128, 4], F32)
        gop = sb("gop", [128, 4], F32)
        pg = sb("pg", [128, 4], F32)
        goa = sb("goa", [128, 4], F32)
        ident_f = sb("ident_f", [128, 128], F32)
        ident_b = sb("ident_b", [128, 128], BF16)
        ones_b = sb("ones_b", [128, 128], BF16)
        ones_f = sb("ones_f", [128, 128], F32)
        iota = sb("iota", [128, 512], F32)
        freq = sb("freq", [128, 1], F32)
        invc = sb("invc", [128, 4, 16], F32)
        maskc = sb("maskc", [128, 4], F32)
        ringA = sb("ringA", [128, 3, 2 * 8 * 128], BF16)
        ringB = sb("ringB", [128, 8, D], BF16)
        wuq = ringA[:, 0, 0:1536].rearrange("p (c n) -> p c n", c=2)
        wuk_sb = ringA[:, 0, 1536:2048]
        wuv_sb = ringA[:, 1, 0:512].rearrange("p (h d) -> p h d", d=DN)
        xres = sb("xres", [128, 4, D], F32)
        xs = sb("xs", [128, 2, D], BF16)
        hT = sb("hT", [128, 8, 512], BF16)
        bb = sb("bb", [128, 28, 512], BF16)
        ff = sb("ff", [128, 8, UW], F32)
        uext = sb("uext", [128, 4, UW], F32)
        PT = sb("PT", [128, 5, 512], BF16)
        sq = sb("sq", [128, 2, 512], BF16)
        accP = sb("accP", [128, 2, 512], F32)
        ckv_out = sb("ckv_out", [128, 4, R], F32)
        kr_out = sb("kr_out", [128, 4, DR], F32)
        pst = ff[:, 2, 0:PW]
        spsb = ff[:, 3, 0:PW]
        stat = sb("stat", [128, 16], F32)
        itile = sb("itile", [128, 512], mybir.dt.int32)

        pb = [ps(f"pb{i}", [128, 512], F32) for i in range(7)]
        psT = ps("psT", [128, 8, 128], BF16)

        def PS(i):
            return f"ps:{i}"

        def mm(out, lhsT, rhs, start, stop, r, w, tp=None):
            if tp is None:
                fn = lambda e: e.matmul(out, lhsT=lhsT, rhs=rhs, start=start, stop=stop)
            else:
                fn = lambda e: e.matmul(out, lhsT=lhsT, rhs=rhs, start=start, stop=stop,
                                        tile_position=tp)
            return T.add("pe", fn, r, w)

        def tr(out, in_, ident, r, w):
            return T.add("pe", lambda e: e.transpose(out=out, in_=in_, identity=ident), r, w)

        def act(out, in_, func, r, w, scale=None, accum=None):
            kw = {}
            if scale is not None:
                kw["scale"] = scale
            if accum is not None:
                kw["accum_out"] = accum
            return T.add("act", lambda e: e.activation(out=out, in_=in_, func=func, **kw), r, w)

        def ts(out, in0, s1, s2, op0, op1, r, w):
            if op1 is None:
                fn = lambda e: e.tensor_scalar(out=out, in0=in0, scalar1=s1, scalar2=None, op0=op0)
            else:
                fn = lambda e: e.tensor_scalar(out=out, in0=in0, scalar1=s1, scalar2=s2,
                                               op0=op0, op1=op1)
            return T.add("dve", fn, r, w)

        def tt(out, in0, in1, op, r, w):
            return T.add("dve", lambda e: e.tensor_tensor(out=out, in0=in0, in1=in1, op=op), r, w)

        def stt(out, in0, scalar, in1, op0, op1, r, w):
            return T.add("dve", lambda e: e.scalar_tensor_tensor(
                out=out, in0=in0, scalar=scalar, in1=in1, op0=op0, op1=op1), r, w)

        def cp(out, in_, r, w):
            return T.add("dve", lambda e: e.tensor_copy(out=out, in_=in_), r, w)

        def memset(out, val, w):
            return T.add("dve", lambda e: e.memset(out, val), (), w)

        def recip(out, in_, r, w):
            return T.add("dve", lambda e: e.reciprocal(out=out, in_=in_), r, w)

        def dma(q, out, in_, r, w, key):
            return T.add(q, lambda e: e.dma_start(out=out, in_=in_), r, w, dkey=key)

        def rstd_chain(buf, src, scale, rsrc, rname):
            ts(buf, src, scale, EPS, ALU.mult, ALU.add, rsrc, [rname])
            if LNEXP:
                act(buf, buf, AF.Ln, [rname], [rname])
                act(buf, buf, AF.Exp, [rname], [rname], scale=-0.5)
            else:
                act(buf, buf, AF.Sqrt, [rname], [rname])
                recip(buf, buf, [rname], [rname])

        T.bulk.update(["cast", "wres", "cres"])
        dma("pool", s_gu[:, :], w_gu_l[:, :], (), ["s_gu"], "cast")
        dma("pool", s_down[:, :], w_down_l[:, :], (), ["s_down"], "cast")
        dma("pool", s_wo[:, :], w_o_l[:, :], (), ["s_wo"], "cast")
        dma("pool", win[:].rearrange("p c n -> p (c n)"), w_in_l[:, :], (), ["win"], "wres")
        dma("pool", ringA[:, 0, 0:1536], w_uq_l[:, :], (), ["wuq"], "wres")
        dma("pool", wuk_sb, w_uk_l[:, :], (), ["wuk_sb"], "wres")
        dma("pool", ringA[:, 1, 0:512], w_uv_l[:, :], (), ["wuv_sb"], "wres")
        dma("pool", wpool[:].rearrange("p g d -> p (g d)"), w_pool_l[:, :], (), ["wpool"], "wres")
        for (t_, src, nm) in ((g1, g1_l, "g1"), (g2, g2_l, "g2"), (gq, gq_l, "gq"),
                              (gkv, gkv_l, "gkv"), (pscale, pscale_l, "pscale"),
                              (gop, gop_l, "gop"), (goa, goa_l, "goa"),
                              (ident_f, c_ident, "ident_f"), (iota, c_iota, "iota"),
                              (freq, c_freq, "freq"), (maskc, c_mask, "maskc")):
            dma("sp", t_[:], src[:, :], (), [nm], "cres")
        dma("sp", invc[:].rearrange("p g t -> p (g t)"), c_invc[:, :], (), ["invc"], "cres")
        dma("sp", gf_bc[:], g_final.partition_broadcast(128), (), ["gf_bc"], "cres")

        memset(ones_b[:], 1.0, ["ones_b"])
        memset(ones_f[:], 1.0, ["ones_f"])
        cp(ident_b[:], ident_f[:], ["ident_f"], ["ident_b"])
        tt(pg[:], pscale[:], gop[:], ALU.mult, ["pscale", "gop"], ["pg"])
        for rep in range(4):
            cp(wkr[:, :, rep * 32:(rep + 1) * 32], win[:, :, 384:416], ["win"], ["wkr"])
            ts(wkrB[:, :, rep * 32:rep * 32 + 16], win[:, :, 400:416], -1.0, None, ALU.mult, None,
               ["win"], ["wkrB"])
            cp(wkrB[:, :, rep * 32 + 16:rep * 32 + 32], win[:, :, 384:400], ["win"], ["wkrB"])
        for c in range(2):
            wq4 = wuq[:, c, :].rearrange("p (h e) -> p h e", e=96)
            cp(wuqN[:, c, :].rearrange("p (h e) -> p h e", e=64), wq4[:, :, 0:64], ["wuq"], ["wuqN"])
            cp(wuqA[:, c, :].rearrange("p (h e) -> p h e", e=32), wq4[:, :, 64:96], ["wuq"], ["wuqA"])
            wB = wuqB[:, c, :].rearrange("p (h e) -> p h e", e=32)
            ts(wB[:, :, 0:16], wq4[:, :, 80:96], -1.0, None, ALU.mult, None, ["wuq"], ["wuqB"])
            cp(wB[:, :, 16:32], wq4[:, :, 64:80], ["wuq"], ["wuqB"])
        for m in range(4):
            tr(psT[:, m, :], wuk_sb[:, m * 128:(m + 1) * 128], ident_b[:], ["wuk_sb", "ident_b"], ["psT"])
        cp(wukT[:], psT[:, 0:4, :], ["psT"], ["wukT"])
        memset(wuvpad[:], 0.0, ["wuvpad"])
        for half in range(2):
            cp(wuvpad[:].rearrange("p (m t) c -> p m t c", t=2)[:, :, half, half * 64:(half + 1) * 64],
               wuv_sb.rearrange("p (m t) d -> p m t d", t=2)[:, :, half, :],
               ["wuv_sb", "wuvpad"], ["wuvpad"])

        store_ops = []

        def sumsq_tiles(tiles, TP, statcol, src=None):
            lo, hi = statcol + tiles[0], statcol + tiles[-1] + 1
            cols = stat[:TP, lo:hi]
            sr = f"stat:{lo}"
            junk = sq[:].rearrange("p a b -> p (a b)")[:TP, :]
            if USE_ACCUM:
                memset(cols, 0.0, [sr])
                for i in tiles:
                    xin, xr = (src[i][:2] if (src and i in src) else (xres[:TP, i, :], [f"xres:{i}"]))
                    jk = junk.rearrange("p (a b) -> p a b", a=2) if (src and i in src and src[i][2]) else junk
                    act(jk, xin, AF.Square, xr + [sr], ["sq:0", "sq:1", sr],
                        accum=stat[:TP, statcol + i:statcol + i + 1])
            else:
                for i in tiles:
                    jf = ff[:, 6:8, :].rearrange("p a b -> p (a b)")[:TP, 0:D]
                    act(jf, xres[:TP, i, :], AF.Square, [f"xres:{i}"], ["ff:6", "ff:7"])
                    T.add("dve", lambda e, o=stat[:TP, statcol + i:statcol + i + 1], j=jf:
                          e.reduce_sum(out=o, in_=j, axis=mybir.AxisListType.X), ["ff:6", "ff:7"], [sr])
            rstd_chain(cols, cols, 1.0 / D, [sr], sr)
            return sr

        def norm_transpose(NT, TP, TB, gvec, gname, statcol, tiles=None, src=None):
            if tiles is None:
                tiles = list(range(NT))
            if not tiles:
                return
            sr = sumsq_tiles(tiles, TP, statcol, src)
            for i in tiles:
                sc = stat[:TP, statcol + i:statcol + i + 1]
                slot = i % 2
                xin, xr = (src[i][:2] if (src and i in src) else (xres[:TP, i, :], [f"xres:{i}"]))
                xo = xs[:TP, slot, :]
                if src and i in src and src[i][2]:
                    xo = xo.rearrange("p (a b) -> p a b", a=2)
                act(xo, xin, AF.Copy, xr + [sr], [f"xs:{slot}"], scale=sc)
                for c in range(8):
                    tr(psT[:, c, :TP], xs[:TP, slot, c * 128:(c + 1) * 128], ident_b[:TP, :TP],
                       [f"xs:{slot}", "ident_b"], ["psT"])
                for c in range(8):
                    ts(hT[:, c, i * TP:(i + 1) * TP], psT[:, c, :TP], gvec[:, c:c + 1], None, ALU.mult, None,
                       ["psT", gname], [f"hT:{c}"])

        ring_state = {"A": 0, "B": 0}
        rope_state = {}
        stage_pending = {}

        def proj_tokmajor(NT, TP, in_chunks, in_names, w_scr, wname, epilogue, after_pass=None, mid_last=None):
            nf = len(in_chunks)
            for p0 in range(0, NT, 2):
                tiles = list(range(p0, min(p0 + 2, NT)))
                for f in range(nf):
                    s = ring_state["B"] % 8
                    ring_state["B"] += 1
                    dma("sp", ringB[:, s, :], w_scr[f * 128:(f + 1) * 128, :], [wname], [f"rB:{s}"], f"rB:{s}")
                    for ti, i in enumerate(tiles):
                        for n in range(2):
                            bk = ti * 2 + n
                            mm(pb[bk][:TP, :], in_chunks[f][:, i * TP:(i + 1) * TP],
                               ringB[:, s, n * 512:(n + 1) * 512], f == 0, f == nf - 1,
                               [in_names[f], f"rB:{s}"], [PS(bk)])
                for ti, i in enumerate(tiles):
                    for n in range(2):
                        epilogue(i, n, pb[ti * 2 + n], PS(ti * 2 + n))
                if mid_last is not None and p0 + 2 >= NT:
                    mid_last()
                if after_pass is not None:
                    after_pass(tiles)

        def block(x_dram, y_dram, ockv, okr, opool, NT, TP, pos0, kcol0, nk_prev_tiles, causal_blk,
                  first_prompt, write_pool, load_tiles, next_x, pre_done=(), early_next=False, next_pos0=None):
            TB = NT * TP
            kt0 = kcol0 // 128
            kb = kcol0 // 512
            Kw = [f"Klat:{kb}", f"Krope:{kb}", f"V:{kb}"]
            for i in load_tiles:
                dma("pool", xres[:TP, i, :], x_dram[i * TP:(i + 1) * TP, :], (), [f"xres:{i}"], f"ldx:{i}")
            def emit_rope(p0, tb):
                Ct_, St_, tmp = ff[:, 0, :tb], ff[:, 1, :tb], ff[:, 2, :tb]
                ts(tmp, iota[:, :tb], float(p0), None, ALU.add, None, ["iota"], ["ff:2"])
                ts(tmp, tmp, freq[:, 0:1], None, ALU.mult, None, ["ff:2", "freq"], ["ff:2"])

                def sin_table(dst, dname, shift):
                    if shift != 0.0:
                        ts(dst, tmp, shift, None, ALU.add, None, ["ff:2"], [dname])
                        src, sname = dst, dname
                    else:
                        src, sname = tmp, "ff:2"
                    ts(itile[:, :tb], src, 1.0 / (2 * PI), None, ALU.mult, None, [sname], ["itile"])
                    stt(dst, itile[:, :tb], -2 * PI, src, ALU.mult, ALU.add, ["itile", sname], [dname])
                    ts(dst, dst, -PI, PI, ALU.max, ALU.min, [dname], [dname])
                    act(dst, dst, AF.Sin, [dname], [dname])
                sin_table(St_, "ff:1", 0.0)
                sin_table(Ct_, "ff:0", 0.5 * PI)
                rope_state["pos"] = p0
            Ct, St = ff[:, 0, :TB], ff[:, 1, :TB]
            if rope_state.get("pos") != pos0:
                emit_rope(pos0, TB)
            if STAGE < 2:
                return
            norm_transpose(NT, TP, TB, g1, "g1", 0, [i for i in range(NT) if i not in pre_done])
            hr = [f"hT:{c}" for c in range(8)]
            if STAGE < 3:
                return
            if SUB < 19:
                return
            cqT = [bb[:, 18, :TB], bb[:, 19, :TB]]
            rq = ff[:, 3, :TB]
            for m in range(2):
                for c in range(8):
                    mm(pb[m][:, :TB], win[:, c, m * 128:(m + 1) * 128], hT[:, c, :TB], c == 0, c == 7,
                       ["win", hr[c]], [PS(m)])
                ts(cqT[m], pb[m][:, :TB], gq[:, m:m + 1], None, ALU.mult, None, [PS(m), "gq"], [f"bb:{18 + m}"])
                if VV >= 12:
                    act(sq[:, m, :TB], pb[m][:, :TB], AF.Square, [PS(m)], [f"sq:{m}"])
            for c in range(8):
                mm(pb[2][:, :TB], win[:, c, 256:384], hT[:, c, :TB], c == 0, c == 7, ["win", hr[c]], [PS(2)])
            for c in range(8):
                mm(pb[3][:, :TB], wkr[:, c, :], hT[:, c, :TB], c == 0, c == 7, ["wkr", hr[c]], [PS(3)])
            for m in range(2):
                if VV >= 13:
                    mm(pb[4][:, :TB], ones_b[:], sq[:, m, :TB], m == 0, m == 1, ["ones_b", f"sq:{m}"], [PS(4)])
            if VV >= 14:
                rstd_chain(rq, pb[4][:, :TB], 1.0 / QL, [PS(4)], "ff:3")
            if SUB < 20:
                return
            if 2 in stage_pending:
                stage_pending.pop(2)()
            ckvT = ff[:, 4, :TB]
            rkv = ff[:, 5, :TB]
            act(sq[:, 0, :TB], pb[2][:, :TB], AF.Square, [PS(2)], ["sq:0"])
            mm(pb[5][:, :TB], ones_b[:], sq[:, 0, :TB], True, True, ["ones_b", "sq:0"], [PS(5)])
            rstd_chain(rkv, pb[5][:, :TB], 1.0 / R, [PS(5)], "ff:5")
            stt(ckvT, pb[2][:, :TB], gkv[:, 0:1], rkv, ALU.mult, ALU.mult, [PS(2), "gkv", "ff:5"], ["ff:4"])
            act(KlatT[:, kcol0:kcol0 + TB], ckvT, AF.Copy, ["ff:4"], [Kw[0]])
            for i in range(NT):
                tr(pb[6][:TP, i * 128:(i + 1) * 128], ckvT[:, i * TP:(i + 1) * TP], ident_f[:], ["ff:4", "ident_f"],
                   [PS(6)])
            cp(ckv_out[:TP, :NT, :], pb[6][:TP, :NT * 128].rearrange("p (i d) -> p i d", d=128), [PS(6)],
               ["ckv_out"])
            act(Vt[:TP, kt0:kt0 + NT, :], pb[6][:TP, :NT * 128].rearrange("p (i d) -> p i d", d=128), AF.Copy,
                [PS(6)], [Kw[2]])
            store_ops.append(dma("pool", ockv.rearrange("(i p) d -> p i d", p=TP), ckv_out[:TP, :NT, :],
                                 ["ckv_out"], (), "st_ckv"))
            if SUB < 21:
                return
            krT = ff[:, 4, :TB]
            t2 = ff[:, 5, :TB]
            for c in range(8):
                mm(pb[1][:, :TB], wkrB[:, c, :], hT[:, c, :TB], c == 0, c == 7, ["wkrB", hr[c]], [PS(1)])
            tt(krT, pb[3][:, :TB], Ct, ALU.mult, [PS(3), "ff:0"], ["ff:4"])
            tt(t2, pb[1][:, :TB], St, ALU.mult, [PS(1), "ff:1"], ["ff:5"])
            tt(krT, krT, t2, ALU.add, ["ff:4", "ff:5"], ["ff:4"])
            act(KropeT[:, kcol0:kcol0 + TB], krT, AF.Copy, ["ff:4"], [Kw[1]])
            for i in range(NT):
                tr(pb[6][:TP, i * 128:(i + 1) * 128], krT[:, i * TP:(i + 1) * TP], ident_f[:], ["ff:4", "ident_f"],
                   [PS(6)])
            cp(kr_out[:TP, :NT, :], pb[6][:TP, :NT * 128].rearrange("p (i d) -> p i d", d=128)[:, :, 0:DR],
               [PS(6)], ["kr_out"])
            store_ops.append(dma("pool", okr.rearrange("(i p) d -> p i d", p=TP), kr_out[:TP, :NT, :],
                                 ["kr_out"], (), "st_kr"))
            if SUB < 22:
                return
            for i_ in (0, 1):
                if i_ in stage_pending:
                    stage_pending.pop(i_)()
            for g in range(4):
                bk = 2 + (g % 2)
                for c in range(8):
                    mm(pb[bk][:, :TB], win[:, c, 416 + g * 128:416 + (g + 1) * 128], hT[:, c, :TB], c == 0, c == 7,
                       ["win", hr[c]], [PS(bk)])
                act(uext[:, g, EXT:EXT + TB], pb[bk][:, :TB], AF.Copy, [PS(bk)], [f"uext:{g}"])
            if STAGE < 4:
                return
            if 3 in stage_pending:
                stage_pending.pop(3)()
            opT = [bb[:, 10 + g, :TB] for g in range(4)]
            for g, wd in enumerate(WINDOWS):
                ur = f"uext:{g}"
                W = EXT + TB
                cur, curr = uext[:, g, :], ur
                bufs = [(ff[:, 6, :], "ff:6"), (ff[:, 7, :], "ff:7")]
                k, bi = 1, 0
                while k < wd:
                    lo = EXT - (wd - 2 * k)
                    dst, dr = bufs[bi]
                    if POOL_MIX:
                        T.add("pool", lambda e, o=dst[:, lo:W], a_=cur[:, lo:W], b_=cur[:, lo - k:W - k]:
                              e.tensor_tensor(out=o, in0=a_, in1=b_, op=ALU.add), [curr], [dr])
                    else:
                        tt(dst[:, lo:W], cur[:, lo:W], cur[:, lo - k:W - k], ALU.add, [curr], [dr])
                    cur, curr = dst, dr
                    bi ^= 1
                    k *= 2
                pooled = bb[:, g, :TB]
                stt(pooled, cur[:, EXT:W], 1.0 / wd, uext[:, g, EXT:W], ALU.mult, ALU.subtract, [curr, ur],
                    [f"bb:{g}"])
                if first_prompt:
                    t16 = ff[:, 2, 0:16]
                    tt(t16, cur[:, EXT:EXT + 16], invc[:, g, :], ALU.mult, [curr, "invc"], ["ff:2"])
                    tt(pooled[:, 0:16], t16, uext[:, g, EXT:EXT + 16], ALU.subtract, ["ff:2", ur], [f"bb:{g}"])
                bk = g % 2
                mm(pb[bk][:, :TB], wpool[:, g, :], pooled, True, True, ["wpool", f"bb:{g}"], [PS(bk)])
                ts(opT[g], pb[bk][:, :TB], pg[:, g:g + 1], None, ALU.mult, None, [PS(bk), "pg"], [f"bb:{10 + g}"])
                act(sq[:, g % 2, :TB], pb[bk][:, :TB], AF.Square, [PS(bk), "pscale"], [f"sq:{g % 2}"],
                    scale=pscale[:, g:g + 1])
                mm(pb[4][:, :TB], ones_b[:], sq[:, g % 2, :TB], g == 0, g == 3, ["ones_b", f"sq:{g % 2}"], [PS(4)])
            rp = ff[:, 5, :TB]
            rstd_chain(rp, pb[4][:, :TB], 1.0 / PW, [PS(4)], "ff:5")
            for g in range(4):
                tt(opT[g], opT[g], rp, ALU.mult, [f"bb:{10 + g}", "ff:5"], [f"bb:{10 + g}"])
            if write_pool:
                for g in range(4):
                    tr(pb[6][:15, g * 128:(g + 1) * 128], uext[:, g, EXT + TB - 15:EXT + TB], ident_f[:],
                       [f"uext:{g}", "ident_f"], [PS(6)])
                cp(pst[:15, :], pb[6][:15, :], [PS(6)], ["ff:2"])
                store_ops.append(dma("pool", opool[:, :], pst[:15, :], ["ff:2"], (), "st_pool"))
            else:
                cp(uext[:, :, 1:16], uext[:, :, TB + 1:TB + 16], [f"uext:{g}" for g in range(4)],
                   [f"uext:{g}" for g in range(4)])
            if STAGE < 5:
                return
            qn = [bb[:, 14 + m, :TB] for m in range(4)]
            for m in range(4):
                bk = m % 2
                for c in range(2):
                    mm(pb[bk][:, :TB], wuqN[:, c, m * 128:(m + 1) * 128], cqT[c], c == 0, c == 1,
                       ["wuqN", f"bb:{18 + c}"], [PS(bk)])
                tt(qn[m], pb[bk][:, :TB], rq, ALU.mult, [PS(bk), "ff:3"], [f"bb:{14 + m}"])
            batch_heads = not causal_blk
            if batch_heads:
                QlatT = [bb[:, 0, h * TB:(h + 1) * TB] for h in range(NH)]
                QLR = ["bb:0"] * NH
            else:
                QlatT = [bb[:, h, :TB] for h in range(NH)]
                QLR = [f"bb:{h}" for h in range(NH)]
            for h in range(NH):
                m, half = h // 2, h % 2
                bk = 2 + (h % 2)
                mm(pb[bk][:, :TB], wukT[half * 64:(half + 1) * 64, m, :], qn[m][half * 64:(half + 1) * 64, :],
                   True, True, ["wukT", f"bb:{14 + m}"], [PS(bk)])
                act(QlatT[h], pb[bk][:, :TB], AF.Copy, [PS(bk)], [QLR[h]])
            QRC = [8, 9, 20, 21, 22, 23, 24, 25]
            if batch_heads:
                QropeM = [bb[:, 8, h * TB:(h + 1) * TB] for h in range(NH)]
                QRR = ["bb:8"] * NH
            else:
                QropeM = [bb[:, QRC[h], :TB] for h in range(NH)]
                QRR = [f"bb:{QRC[h]}" for h in range(NH)]
            for a in range(2):
                for c in range(2):
                    mm(pb[0][:, :TB], wuqA[:, c, a * 128:(a + 1) * 128], cqT[c], c == 0, c == 1,
                       ["wuqA", f"bb:{18 + c}"], [PS(0)])
                for c in range(2):
                    mm(pb[1][:, :TB], wuqB[:, c, a * 128:(a + 1) * 128], cqT[c], c == 0, c == 1,
                       ["wuqB", f"bb:{18 + c}"], [PS(1)])
                t1, t2 = ff[:, 6, :TB], ff[:, 7, :TB]
                tt(t1, pb[0][:, :TB], Ct, ALU.mult, [PS(0), "ff:0"], ["ff:6"])
                tt(t2, pb[1][:, :TB], St, ALU.mult, [PS(1), "ff:1"], ["ff:7"])
                tt(t1, t1, t2, ALU.add, ["ff:6", "ff:7"], ["ff:6"])
                tt(t1, t1, rq, ALU.mult, ["ff:6", "ff:3"], ["ff:6"])
                for g in range(4):
                    ts(QropeM[4 * a + g], t1, maskc[:, g:g + 1], None, ALU.mult, None, ["ff:6", "maskc"],
                       [QRR[4 * a + g]])
            if STAGE < 6:
                return
            ktiles = []
            for j in range(nk_prev_tiles):
                ktiles.append((j * 128, j, 128, 0, False, j // 4))
            if causal_blk:
                for m in range(NT):
                    ktiles.append((kcol0 + m * 128, kt0 + m, 128, m * 128, True, kb))
            else:
                ktiles.append((kcol0, kt0, TB, 0, False, kb))
            oaT = [bb[:, 14 + m, :TB] for m in range(4)]
            LA = 2
            SB_ = (0, 1, 2)
            NPT = 5
            nk = len(ktiles)
            vheads = [list(range(NH))] if batch_heads else [[h] for h in range(NH)]
            W = TB * len(vheads[0])
            if batch_heads:
                QLv = [bb[:, 0, :W]]
                QRv = [bb[:, 8, :W]]
                QLn, QRn = [["bb:0"]], [["bb:8"]]
            else:
                QLv, QRv = QlatT, QropeM
                QLn, QRn = [[x] for x in QLR], [[x] for x in QRR]
            jobs = [(v, ki) for v in range(len(vheads)) for ki in range(nk)]
            pend = []

            def do_pv(v, ki, psl):
                kc, vt, kr_, n0, diag, kblk = ktiles[ki]
                ob = 3 + (v % 2)
                mm(pb[ob][:, n0:W], Vt[:kr_, vt, :], PT[:kr_, psl, n0:W], ki == 0, ki == nk - 1,
                   [f"V:{kblk}", f"PT:{psl}"], [PS(ob)])

            def epi1(v):
                ob = 3 + (v % 2)
                acc = ff[:, 6 + (v % 2), :W]
                accr = f"ff:{6 + (v % 2)}"
                mm(pb[5][:, :W], ones_f[:], acc, True, False, ["ones_f", accr], [PS(5)])
                mm(pb[5][:, :W], ones_f[:], accP[:, v % 2, :W], False, True, ["ones_f", f"accP:{v % 2}"], [PS(5)])
                rden = ff[:, 4 + (v % 2), :W]
                rdr = f"ff:{4 + (v % 2)}"
                act(rden, pb[5][:, :W], AF.Ln, [PS(5)], [rdr])
                act(rden, rden, AF.Exp, [rdr], [rdr], scale=-1.0)
                olat = sq[:, v % 2, :W]
                tt(olat, pb[ob][:, :W], rden, ALU.mult, [PS(ob), rdr], [f"sq:{v % 2}"])

            def epi2(v):
                for j, h in enumerate(vheads[v]):
                    m = h // 2
                    olat = sq[:, v % 2, j * TB:(j + 1) * TB]
                    mm(pb[6][:, :TB], wuvpad[:, h, :], olat, h % 2 == 0, h % 2 == 1, ["wuvpad", f"sq:{v % 2}"],
                       [PS(6)])
                    if h % 2 == 1:
                        act(oaT[m], pb[6][:, :TB], AF.Copy, [PS(6)], [f"bb:{14 + m}"])

            def flush(upto):
                keep = []
                for (at, fn) in pend:
                    if at <= upto:
                        fn()
                    else:
                        keep.append((at, fn))
                pend[:] = keep

            for idx, (v, ki) in enumerate(jobs):
                kc, vt, kr_, n0, diag, kblk = ktiles[ki]
                sb_ = SB_[idx % len(SB_)]
                psl = idx % NPT
                acc = ff[:, 6 + (v % 2), :W]
                accr = f"ff:{6 + (v % 2)}"
                mm(pb[sb_][:kr_, n0:W], KlatT[:, kc:kc + kr_], QLv[v][:, n0:W], True, False,
                   [f"Klat:{kblk}"] + QLn[v], [PS(sb_)])
                mm(pb[sb_][:kr_, n0:W], KropeT[:, kc:kc + kr_], QRv[v][:, n0:W], False, True,
                   [f"Krope:{kblk}"] + QRn[v], [PS(sb_)])
                act(PT[:kr_, psl, n0:W], pb[sb_][:kr_, n0:W], AF.Exp, [PS(sb_)], [f"PT:{psl}"],
                    scale=SM_SCALE)
                if diag:
                    memset(PT[64:128, psl, n0:n0 + 64], 0.0, [f"PT:{psl}"])
                if ki == 0:
                    cp(acc, PT[:, psl, :W], [f"PT:{psl}"], [accr])
                    memset(accP[:, v % 2, :W], 0.0, [f"accP:{v % 2}"])
                elif ki % 2 == 1:
                    a2 = accP[:, v % 2, :W]
                    tt(a2[:kr_, n0:W], a2[:kr_, n0:W], PT[:kr_, psl, n0:W], ALU.add,
                       [f"accP:{v % 2}", f"PT:{psl}"], [f"accP:{v % 2}"])
                else:
                    tt(acc[:kr_, n0:W], acc[:kr_, n0:W], PT[:kr_, psl, n0:W], ALU.add,
                       [accr, f"PT:{psl}"], [accr])
                pend.append((idx + LA, lambda v=v, ki=ki, psl=psl: do_pv(v, ki, psl)))
                if ki == nk - 1:
                    pend.append((idx + LA + 1, lambda v=v: epi1(v)))
                    pend.append((idx + LA + 4, lambda v=v: epi2(v)))
                flush(idx)
            flush(10 ** 9)
            for m in range(4):
                act(PT[:, m, :TB], oaT[m], AF.Square, [f"bb:{14 + m}"], [f"PT:{m}"])
            for m in range(4):
                mm(pb[2][:, :TB], ones_b[:], PT[:, m, :TB], m == 0, m == 3, ["ones_b", f"PT:{m}"], [PS(2)])
            ra = ff[:, 4, :TB]
            rstd_chain(ra, pb[2][:, :TB], 1.0 / PW, [PS(2)], "ff:4")
            for m in range(4):
                stt(oaT[m], oaT[m], goa[:, m:m + 1], ra, ALU.mult, ALU.mult, [f"bb:{14 + m}", "goa", "ff:4"],
                    [f"bb:{14 + m}"])
            if STAGE < 7:
                return
            def epi_res(i, n, pst_, psr):
                tt(xres[:TP, i, n * 512:(n + 1) * 512], pst_[:TP, :], xres[:TP, i, n * 512:(n + 1) * 512], ALU.add,
                   [psr, f"xres:{i}"], [f"xres:{i}"])
            if NT == 4 and SPLIT_NORM2:
                proj_tokmajor(NT, TP, oaT + opT,
                              [f"bb:{14 + m}" for m in range(4)] + [f"bb:{10 + g}" for g in range(4)],
                              s_wo, "s_wo", epi_res, None,
                              lambda: norm_transpose(NT, TP, TB, g2, "g2", 4, [0, 1]))
                n2_tiles = [2, 3]
            else:
                proj_tokmajor(NT, TP, oaT + opT,
                              [f"bb:{14 + m}" for m in range(4)] + [f"bb:{10 + g}" for g in range(4)],
                              s_wo, "s_wo", epi_res)
                n2_tiles = None
            if STAGE < 8:
                return
            norm_transpose(NT, TP, TB, g2, "g2", 4, n2_tiles)
            if STAGE < 9:
                return
            actT = [bb[:, f, :TB] for f in range(NF)]
            use_stage = bool(STAGE_X and early_next and next_x is not None and NT == 4)
            stgA = ff[:, 4:6, :].rearrange("p a b -> p (a b)")[:, 0:D]
            stgB = ff[:, 6:8, :].rearrange("p a b -> p (a b)")[:, 0:D]
            stg_src = {0: (uext[:, 0:2, EXT:EXT + 512], ["uext:0", "uext:1"], True),
                       1: (uext[:, 2:4, EXT:EXT + 512], ["uext:2", "uext:3"], True),
                       2: (stgA, ["ff:4", "ff:5"], False), 3: (stgB, ["ff:6", "ff:7"], False)}
            if EARLY_ROPE and next_pos0 is not None:
                emit_rope(next_pos0, 512)
            if use_stage:
                dma("pool", stgA, next_x[256:384, :], (), ["ff:4", "ff:5"], "ldxs:2")
                for i in (0, 1):
                    dma("pool", stg_src[i][0],
                        next_x[i * 128:(i + 1) * 128, :].rearrange("p (a b) -> p a b", a=2), (), stg_src[i][1],
                        f"ldxs:{i}")
            for f in range(NF):
                s = ring_state["A"] % 3
                ring_state["A"] += 1
                dma("sp", ringA[:, s, :], s_gu[f * 128:(f + 1) * 128, :], ["s_gu"],
                    [f"rA:{s}"] + {0: ["wuq", "wuk_sb"], 1: ["wuv_sb"]}.get(s, []), f"rA:{s}")
                wv = ringA[:, s, :].rearrange("p (t c n) -> p t c n", t=2, c=8)
                bg, bu = (f % 2) * 2, (f % 2) * 2 + 1
                for c in range(8):
                    mm(pb[bg][:, :TB], wv[:, 0, c, :], hT[:, c, :TB], c == 0, c == 7, [f"rA:{s}", hr[c]], [PS(bg)])
                for c in range(8):
                    mm(pb[bu][:, :TB], wv[:, 1, c, :], hT[:, c, :TB], c == 0, c == 7, [f"rA:{s}", hr[c]], [PS(bu)])
                sg = ff[:, 6 + (f % 2), :TB]
                sgr = f"ff:{6 + (f % 2)}"
                act(sg, pb[bg][:, :TB], AF.Silu, [PS(bg)], [sgr])
                tt(actT[f], sg, pb[bu][:, :TB], ALU.mult, [sgr, PS(bu)], [f"bb:{f}"])
            if use_stage:
                dma("pool", stgB, next_x[384:512, :], (), ["ff:6", "ff:7"], "ldxs:3")

            def final_tiles(tiles):
                sr = sumsq_tiles(tiles, TP, 8)
                for i in tiles:
                    sc = stat[:TP, 8 + i:9 + i]
                    stt(xres[:TP, i, :], xres[:TP, i, :], sc, gf_bc[:TP, :], ALU.mult, ALU.mult,
                        [f"xres:{i}", sr, "gf_bc"], [f"xres:{i}"])
                    store_ops.append(dma("pool", y_dram[i * TP:(i + 1) * TP, :], xres[:TP, i, :], [f"xres:{i}"], (),
                                         f"sty:{i}"))
                    if next_x is not None:
                        if use_stage:
                            def _cp(i=i):
                                xo = xres[:, i, :]
                                if stg_src[i][2]:
                                    xo = xo.rearrange("p (a b) -> p a b", a=2)
                                act(xo, stg_src[i][0], AF.Copy, stg_src[i][1], [f"xres:{i}"])
                            stage_pending[i] = _cp
                        else:
                            dma("pool", xres[:, i, :], next_x[i * 128:(i + 1) * 128, :], (), [f"xres:{i}"],
                                f"ldx:{i}")
            if STAGE < 10:
                return
            def early_norm():
                if use_stage:
                    norm_transpose(4, 128, 512, g1, "g1", 0, [0, 1, 2, 3], stg_src)
                else:
                    norm_transpose(4, 128, 512, g1, "g1", 0, [0, 1])
            proj_tokmajor(NT, TP, actT, [f"bb:{f}" for f in range(NF)], s_down, "s_down", epi_res, final_tiles,
                          early_norm if (early_next and next_x is not None and NT == 4) else None)

        if DO_SAMPLE and STAGE >= 1:
            kres = [f"Klat:{b}" for b in range(8)]
            rres = [f"Krope:{b}" for b in range(8)]
            vres = [f"V:{b}" for b in range(8)]
            stg = ff[:, 6:8, :].rearrange("p a b -> p (a b)")[:, 0:1024]
            for q in range(4):
                dma("sp", stg.rearrange("p (t r) -> p t r", r=128),
                    cc_kv[q * 1024:(q + 1) * 1024, :].rearrange("(t p) r -> p t r", p=128), (), ["ff:6", "ff:7"],
                    "ldc")
                act(Vt[:, q * 8:(q + 1) * 8, :], stg.rearrange("p (t r) -> p t r", r=128), AF.Copy,
                    ["ff:6", "ff:7"], [vres[2 * q], vres[2 * q + 1]])
            krst = bb[:, 0:8, :].rearrange("p a (t q) -> p (a t) q", q=128)
            stg2 = ff[:, 4:6, :].rearrange("p a b -> p (a b)")[:, 0:1024].rearrange("p (t d) -> p t d", d=32)
            for q in range(4):
                dma("sp", stg2[:, q * 8:(q + 1) * 8, :],
                    cc_kr[q * 1024:(q + 1) * 1024, :].rearrange("(t p) d -> p t d", p=128), (), ["ff:4", "ff:5"],
                    f"ldk:{q}")
            for rep in range(4):
                cp(krst[:, :, rep * 32:(rep + 1) * 32], stg2, ["ff:4", "ff:5"], [f"bb:{i}" for i in range(8)])
            dma("sp", spsb[:15, :], st_pool[:, :], (), ["ff:3"], "ldsp")
            for t8 in range(4 if SUB >= 1 else 0):
                for j in range(8):
                    t = t8 * 8 + j
                    tr(psT[:, j, :], Vt[:, t, :], ident_b[:], [f"V:{t // 4}", "ident_b"], ["psT"])
                cp(KlatT[:, t8 * 1024:(t8 + 1) * 1024], psT[:].rearrange("p j k -> p (j k)"), ["psT"],
                   [kres[2 * t8], kres[2 * t8 + 1]])
            for t8 in range(4 if SUB >= 2 else 0):
                for j in range(8):
                    t = t8 * 8 + j
                    tr(psT[:, j, :], krst[:, t, :], ident_b[:], [f"bb:{t // 4}", "ident_b"], ["psT"])
                cp(KropeT[:, t8 * 1024:(t8 + 1) * 1024], psT[:].rearrange("p j k -> p (j k)"), ["psT"],
                   [rres[2 * t8], rres[2 * t8 + 1]])
            for g in range(4 if SUB >= 3 else 0):
                tr(pb[6][:, g * 128:g * 128 + 15], spsb[:15, g * 128:(g + 1) * 128], ident_f[:15, :15],
                   ["ff:3", "ident_f"], [PS(6)])
            cp(uext[:, :, 1:16], pb[6][:, :].rearrange("p (g k) -> p g k", k=128)[:, :, 0:15], [PS(6)],
               [f"uext:{g}" for g in range(4)])
            for i in range(1, 4):
                dma("pool", xres[:, i, :], xp[i * 128:(i + 1) * 128, :], (), [f"xres:{i}"], f"ldx:{i}")
            block(xsm, y_s, o_ckv_s, o_kr_s, o_pool_s, 1, DEC, PAST, PAST, 32, False, False, True, [0],
                  xp[0:512, :] if NBLK > 0 else None, next_pos0=(0 if NBLK > 0 else None))
            first_loads = []
        else:
            first_loads = [0, 1, 2, 3]

        memset(uext[:, :, 0:16], 0.0, [f"uext:{g}" for g in range(4)])
        for J in range(NBLK):
            block(xp[J * 512:(J + 1) * 512, :], y_p[J * 512:(J + 1) * 512, :],
                  o_ckv_p[J * 512:(J + 1) * 512, :], o_kr_p[J * 512:(J + 1) * 512, :], o_pool_p,
                  4, 128, J * 512, J * 512, 4 * J, True, J == 0, J == SEQ // 512 - 1,
                  first_loads if J == 0 else [],
                  xp[(J + 1) * 512:(J + 2) * 512, :] if J + 1 < NBLK else None,
                  pre_done=(((0, 1, 2, 3) if STAGE_X else (0, 1)) if (J > 0 and EARLY_NORM) else ()),
                  early_next=bool(EARLY_NORM), next_pos0=((J + 1) * 512 if J + 1 < NBLK else None))

        lastst = {}
        for op in store_ops:
            lastst[op.dkey] = op
        T.add("sp", None, (), (), extra=list(lastst.values()))

        dkeys = T.resolve()
        esem = {e: es.enter_context(nc.semaphore(f"e_{e}")) for e in Tracker.ENGS}
        dsem = {k: es.enter_context(nc.semaphore("d_" + k.replace(":", "_"))) for k in dkeys}
        with nc.Block() as blk:
            @blk.tensor
            def _(e):
                T.emit("pe", e, esem, dsem)

            @blk.scalar
            def _(e):
                T.emit("act", e, esem, dsem)

            @blk.vector
            def _(e):
                T.emit("dve", e, esem, dsem)

            @blk.gpsimd
            def _(e):
                T.emit("pool", e, esem, dsem)

            @blk.sync
            def _(e):
                T.emit("sp", e, esem, dsem)
    return nc


def _chunked(v, n):
    return np.ascontiguousarray(np.asarray(v, np.float32).reshape(n, 128).T)


def kernel(x_prompt, x_sample, cache_kv_latent, cache_k_rope, state_pool,
           g_norm1, w_in, g_q, w_uq, g_kv, w_uk, w_uv, w_pool, pool_scale,
           g_out_attn, g_out_pool, w_o, g_norm2, w_gate, w_up, w_down, g_final):
    f = lambda a: np.ascontiguousarray(np.asarray(a, np.float32))
    w_in0, w_uq0 = f(w_in)[0], f(w_uq)[0]
    shared = {
        "w_in_l": f(w_in0.reshape(8, 128, INW).transpose(1, 0, 2).reshape(128, 8 * INW)),
        "w_uq_l": f(w_uq0.reshape(2, 128, 768).transpose(1, 0, 2).reshape(128, 2 * 768)),
        "w_uk_l": f(f(w_uk)[0].transpose(1, 0, 2).reshape(128, NH * DN)),
        "w_uv_l": f(f(w_uv)[0].transpose(1, 0, 2).reshape(128, NH * DN)),
        "w_pool_l": f(f(w_pool)[0].transpose(1, 0, 2).reshape(128, 4 * 128)),
        "w_down_l": f(f(w_down)[0]),
        "w_o_l": f(f(w_o)[0]),
        "g1_l": _chunked(f(g_norm1)[0], 8), "g2_l": _chunked(f(g_norm2)[0], 8),
        "gq_l": _chunked(f(g_q)[0], 2), "gkv_l": _chunked(f(g_kv)[0], 1),
        "pscale_l": _chunked(f(pool_scale)[0], 4), "gop_l": _chunked(f(g_out_pool)[0], 4),
        "goa_l": _chunked(f(g_out_attn)[0], 4),
        "g_final": f(g_final),
    }
    wg = f(w_gate)[0].reshape(8, 128, NF, 128).transpose(2, 1, 0, 3)
    wu = f(w_up)[0].reshape(8, 128, NF, 128).transpose(2, 1, 0, 3)
    shared["w_gu_l"] = f(np.stack([wg, wu], axis=2).reshape(NF * 128, 2 * 8 * 128))
    shared["c_ident"] = np.eye(128, dtype=np.float32)
    shared["c_iota"] = np.ascontiguousarray(np.broadcast_to(np.arange(512, dtype=np.float32), (128, 512)))
    fr = np.power(np.float32(10000.0), -np.arange(0, DR, 2, dtype=np.float32) / np.float32(DR)).astype(np.float32)
    shared["c_freq"] = np.ascontiguousarray(fr[np.arange(128) % 16].reshape(128, 1))
    invc = np.zeros((4, 16), np.float32)
    for g, wd in enumerate(WINDOWS):
        invc[g] = 1.0 / np.minimum(np.arange(16) + 1, wd).astype(np.float32)
    shared["c_invc"] = np.ascontiguousarray(np.broadcast_to(invc.reshape(1, 64), (128, 64)))
    shared["c_mask"] = np.ascontiguousarray((np.arange(128)[:, None] // 32 == np.arange(4)[None, :]).astype(np.float32))

    xp_, xs_ = f(x_prompt), f(x_sample)
    ckv_, ckr_, stp_ = f(cache_kv_latent)[0], f(cache_k_rope)[0], f(state_pool)[0]
    in_maps = []
    for c in range(NCORES):
        m = dict(shared)
        m.update({"xp": xp_[c], "xsm": xs_[c], "cc_kv": ckv_[c], "cc_kr": ckr_[c], "st_pool": stp_[c]})
        in_maps.append(m)
    nc = build_program()
    res = run_bass_kernel_spmd(nc, in_maps, core_ids=list(range(NCORES)))
    rs = res.results
    st = lambda k: np.stack([np.asarray(rs[c % NCORES][k], np.float32) for c in range(8)], axis=0)
    return (st("y_p"), st("y_s"), st("o_ckv_p")[None], st("o_kr_p")[None], st("o_pool_p")[None],
            st("o_ckv_s")[None], st("o_kr_s")[None], st("o_pool_s")[None])
```

```python
import os
import math
from contextlib import ExitStack
import numpy as np
import concourse.bass as bass
import concourse.mybir as mybir
from concourse.bass_utils import run_bass_kernel_spmd

F32 = mybir.dt.float32
BF16 = mybir.dt.bfloat16
AF = mybir.ActivationFunctionType
ALU = mybir.AluOpType

D = 1024
SEQ = 8192
DEC = 32
PAST = 4096
NH = 8
R = 128
DN = 64
DR = 32
QL = 256
PW = 512
DFF = 2816
NF = DFF // 128
INW = 928
EPS = 1e-6
SM_SCALE = 1.0 / math.sqrt(DN + DR)
WINDOWS = (2, 4, 8, 16)
PI = math.pi
EXT = 16
UW = EXT + 512

NBLK = 16
DO_SAMPLE = 1
STAGE = 99
NCORES = 8
SUB = 99
VV = 99
POOL_ACC = 0
USE_ACCUM = 1
LNEXP = 1
EARLY_NORM = 1
POOL_MIX = 1
SPLIT_NORM2 = 1
STAGE_X = 1
EARLY_ROPE = 1


class _Op:
    __slots__ = ("eng", "fn", "deps", "sig", "cnt", "dkey", "dcnt", "idx")


class Tracker:
    ENGS = ("pe", "act", "dve", "pool", "sp")

    def __init__(self):
        self.ops = []
        self.lw = {}
        self.rd = {}
        self.eng_ops = {e: [] for e in self.ENGS}
        self.bulk = set()
        self.last_dma = {}

    def add(self, eng, fn, r=(), w=(), dkey=None, extra=()):
        op = _Op()
        op.eng = eng
        op.fn = fn
        op.dkey = dkey
        op.sig = False
        op.cnt = 0
        op.dcnt = 0
        deps = {}

        def need(p):
            if p is None:
                return
            if p.dkey is None and dkey is None and p.eng == "pe" and eng == "pe":
                return
            k = p.dkey if p.dkey is not None else p.eng
            q = deps.get(k)
            if q is None or q.idx < p.idx:
                deps[k] = p

        for x in r:
            need(self.lw.get(x))
            if x.startswith("ps"):
                rr = self.rd.get(x)
                if rr:
                    for k_, p in rr.items():
                        if k_ != (dkey if dkey is not None else eng):
                            need(p)
        for x in w:
            need(self.lw.get(x))
            rr = self.rd.get(x)
            if rr:
                for p in rr.values():
                    need(p)
        for p in extra:
            need(p)
        op.deps = list(deps.values())
        op.idx = len(self.ops)
        self.ops.append(op)
        self.eng_ops[eng].append(op)
        sk = dkey if dkey is not None else eng
        for x in r:
            d = self.rd.get(x)
            if d is None:
                d = self.rd[x] = {}
            d[sk] = op
        for x in w:
            self.lw[x] = op
            self.rd[x] = {}
        if dkey is not None:
            self.last_dma[dkey] = op
        return op

    def resolve(self):
        for op in self.ops:
            for p in op.deps:
                if p.dkey is None:
                    p.sig = True
        for e in self.ENGS:
            c = 0
            for op in self.eng_ops[e]:
                if op.dkey is None and op.sig:
                    c += 1
                    op.cnt = c
        tot = {}
        for op in self.ops:
            if op.dkey is not None:
                tot[op.dkey] = tot.get(op.dkey, 0) + 16
                op.dcnt = tot[op.dkey]
        for op in self.ops:
            if op.dkey in self.bulk:
                op.dcnt = tot[op.dkey]
        return sorted(tot.keys())

    def emit(self, eng_name, e, esem, dsem):
        known = {}
        for op in self.eng_ops[eng_name]:
            for p in op.deps:
                if p.dkey is not None:
                    sem, val, k = dsem[p.dkey], p.dcnt, ("d", p.dkey)
                else:
                    sem, val, k = esem[p.eng], p.cnt, ("e", p.eng)
                if known.get(k, 0) < val:
                    e.wait_ge(sem, val)
                    known[k] = val
            if op.fn is None:
                continue
            ins = op.fn(e)
            if op.dkey is not None:
                ins.then_inc(dsem[op.dkey], 16)
            elif op.sig:
                ins.then_inc(esem[op.eng], 1)


def build_program():
    nc = bass.Bass("TRN2", target_bir_lowering=False)
    T = Tracker()

    def din(name, shape, dt=F32):
        return nc.dram_tensor(name, list(shape), dt, kind="ExternalInput").ap()

    def dout(name, shape):
        return nc.dram_tensor(name, list(shape), F32, kind="ExternalOutput").ap()

    def dscr(name, shape, dt=BF16):
        return nc.dram_tensor(name, list(shape), dt, kind="Internal").ap()

    xp = din("xp", [SEQ, D])
    xsm = din("xsm", [DEC, D])
    cc_kv = din("cc_kv", [PAST, R])
    cc_kr = din("cc_kr", [PAST, DR])
    st_pool = din("st_pool", [15, PW])
    w_in_l = din("w_in_l", [128, 8 * INW])
    w_uq_l = din("w_uq_l", [128, 2 * 768])
    w_uk_l = din("w_uk_l", [128, NH * DN])
    w_uv_l = din("w_uv_l", [128, NH * DN])
    w_pool_l = din("w_pool_l", [128, 4 * 128])
    w_gu_l = din("w_gu_l", [NF * 128, 2 * 8 * 128])
    w_down_l = din("w_down_l", [NF * 128, D])
    w_o_l = din("w_o_l", [8 * 128, D])
    g1_l = din("g1_l", [128, 8])
    g2_l = din("g2_l", [128, 8])
    gq_l = din("gq_l", [128, 2])
    gkv_l = din("gkv_l", [128, 1])
    pscale_l = din("pscale_l", [128, 4])
    gop_l = din("gop_l", [128, 4])
    goa_l = din("goa_l", [128, 4])
    g_final = din("g_final", [D])
    c_ident = din("c_ident", [128, 128])
    c_iota = din("c_iota", [128, 512])
    c_freq = din("c_freq", [128, 1])
    c_invc = din("c_invc", [128, 64])
    c_mask = din("c_mask", [128, 4])

    y_p = dout("y_p", [SEQ, D])
    y_s = dout("y_s", [DEC, D])
    o_ckv_p = dout("o_ckv_p", [SEQ, R])
    o_kr_p = dout("o_kr_p", [SEQ, DR])
    o_pool_p = dout("o_pool_p", [15, PW])
    o_ckv_s = dout("o_ckv_s", [DEC, R])
    o_kr_s = dout("o_kr_s", [DEC, DR])
    o_pool_s = dout("o_pool_s", [15, PW])

    s_gu = dscr("s_gu", [NF * 128, 2 * 8 * 128])
    s_down = dscr("s_down", [NF * 128, D])
    s_wo = dscr("s_wo", [8 * 128, D])

    es = ExitStack()
    with es:
        def sb(name, shape, dt):
            return es.enter_context(nc.sbuf_tensor(name, list(shape), dt))

        def ps(name, shape, dt):
            return es.enter_context(nc.psum_tensor(name, list(shape), dt))

        KlatT = sb("KlatT", [128, SEQ], BF16)
        KropeT = sb("KropeT", [128, SEQ], BF16)
        Vt = sb("Vt", [128, SEQ // 128, R], BF16)
        win = sb("win", [128, 8, INW], BF16)
        wkr = sb("wkr", [128, 8, 128], BF16)
        wkrB = sb("wkrB", [128, 8, 128], BF16)
        wuqN = sb("wuqN", [128, 2, 512], BF16)
        wuqA = sb("wuqA", [128, 2, 256], BF16)
        wuqB = sb("wuqB", [128, 2, 256], BF16)
        wukT = sb("wukT", [128, 4, 128], BF16)
        wuvpad = sb("wuvpad", [128, NH, 128], BF16)
        wpool = sb("wpool", [128, 4, 128], BF16)
        gf_bc = sb("gf_bc", [128, D], F32)
        g1 = sb("g1", [128, 8], F32)
        g2 = sb("g2", [128, 8], F32)
        gq = sb("gq", [128, 2], F32)
        gkv = sb("gkv", [128, 1], F32)
        pscale = sb("pscale", [128, 4], F32)
        gop = sb("gop", [128, 4], F32)
        pg = sb("pg", [128, 4], F32)
        goa = sb("goa", [128, 4], F32)
        ident_f = sb("ident_f", [128, 128], F32)
        ident_b = sb("ident_b", [128, 128], BF16)
        ones_b = sb("ones_b", [128, 128], BF16)
        ones_f = sb("ones_f", [128, 128], F32)
        iota = sb("iota", [128, 512], F32)
        freq = sb("freq", [128, 1], F32)
        invc = sb("invc", [128, 4, 16], F32)
        maskc = sb("maskc", [128, 4], F32)
        ringA = sb("ringA", [128, 3, 2 * 8 * 128], BF16)
        ringB = sb("ringB", [128, 8, D], BF16)
        wuq = ringA[:, 0, 0:1536].rearrange("p (c n) -> p c n", c=2)
        wuk_sb = ringA[:, 0, 1536:2048]
        wuv_sb = ringA[:, 1, 0:512].rearrange("p (h d) -> p h d", d=DN)
        xres = sb("xres", [128, 4, D], F32)
        xs = sb("xs", [128, 2, D], BF16)
        hT = sb("hT", [128, 8, 512], BF16)
        bb = sb("bb", [128, 28, 512], BF16)
        ff = sb("ff", [128, 8, UW], F32)
        uext = sb("uext", [128, 4, UW], F32)
        PT = sb("PT", [128, 5, 512], BF16)
        sq = sb("sq", [128, 2, 512], BF16)
        accP = sb("accP", [128, 2, 512], F32)
        ckv_out = sb("ckv_out", [128, 4, R], F32)
        kr_out = sb("kr_out", [128, 4, DR], F32)
        pst = ff[:, 2, 0:PW]
        spsb = ff[:, 3, 0:PW]
        stat = sb("stat", [128, 16], F32)
        itile = sb("itile", [128, 512], mybir.dt.int32)

        pb = [ps(f"pb{i}", [128, 512], F32) for i in range(7)]
        psT = ps("psT", [128, 8, 128], BF16)

        def PS(i):
            return f"ps:{i}"

        def mm(out, lhsT, rhs, start, stop, r, w, tp=None):
            if tp is None:
                fn = lambda e: e.matmul(out, lhsT=lhsT, rhs=rhs, start=start, stop=stop)
            else:
                fn = lambda e: e.matmul(out, lhsT=lhsT, rhs=rhs, start=start, stop=stop,
                                        tile_position=tp)
            return T.add("pe", fn, r, w)

        def tr(out, in_, ident, r, w):
            return T.add("pe", lambda e: e.transpose(out=out, in_=in_, identity=ident), r, w)

        def act(out, in_, func, r, w, scale=None, accum=None):
            kw = {}
            if scale is not None:
                kw["scale"] = scale
            if accum is not None:
                kw["accum_out"] = accum
            return T.add("act", lambda e: e.activation(out=out, in_=in_, func=func, **kw), r, w)

        def ts(out, in0, s1, s2, op0, op1, r, w):
            if op1 is None:
                fn = lambda e: e.tensor_scalar(out=out, in0=in0, scalar1=s1, scalar2=None, op0=op0)
            else:
                fn = lambda e: e.tensor_scalar(out=out, in0=in0, scalar1=s1, scalar2=s2,
                                               op0=op0, op1=op1)
            return T.add("dve", fn, r, w)

        def tt(out, in0, in1, op, r, w):
            return T.add("dve", lambda e: e.tensor_tensor(out=out, in0=in0, in1=in1, op=op), r, w)

        def stt(out, in0, scalar, in1, op0, op1, r, w):
            return T.add("dve", lambda e: e.scalar_tensor_tensor(
                out=out, in0=in0, scalar=scalar, in1=in1, op0=op0, op1=op1), r, w)

        def cp(out, in_, r, w):
            return T.add("dve", lambda e: e.tensor_copy(out=out, in_=in_), r, w)

        def memset(out, val, w):
            return T.add("dve", lambda e: e.memset(out, val), (), w)

        def recip(out, in_, r, w):
            return T.add("dve", lambda e: e.reciprocal(out=out, in_=in_), r, w)

        def dma(q, out, in_, r, w, key):
            return T.add(q, lambda e: e.dma_start(out=out, in_=in_), r, w, dkey=key)

        def rstd_chain(buf, src, scale, rsrc, rname):
            ts(buf, src, scale, EPS, ALU.mult, ALU.add, rsrc, [rname])
            if LNEXP:
                act(buf, buf, AF.Ln, [rname], [rname])
                act(buf, buf, AF.Exp, [rname], [rname], scale=-0.5)
            else:
                act(buf, buf, AF.Sqrt, [rname], [rname])
                recip(buf, buf, [rname], [rname])

        T.bulk.update(["cast", "wres", "cres"])
        dma("pool", s_gu[:, :], w_gu_l[:, :], (), ["s_gu"], "cast")
        dma("pool", s_down[:, :], w_down_l[:, :], (), ["s_down"], "cast")
        dma("pool", s_wo[:, :], w_o_l[:, :], (), ["s_wo"], "cast")
        dma("pool", win[:].rearrange("p c n -> p (c n)"), w_in_l[:, :], (), ["win"], "wres")
        dma("pool", ringA[:, 0, 0:1536], w_uq_l[:, :], (), ["wuq"], "wres")
        dma("pool", wuk_sb, w_uk_l[:, :], (), ["wuk_sb"], "wres")
        dma("pool", ringA[:, 1, 0:512], w_uv_l[:, :], (), ["wuv_sb"], "wres")
        dma("pool", wpool[:].rearrange("p g d -> p (g d)"), w_pool_l[:, :], (), ["wpool"], "wres")
        for (t_, src, nm) in ((g1, g1_l, "g1"), (g2, g2_l, "g2"), (gq, gq_l, "gq"),
                              (gkv, gkv_l, "gkv"), (pscale, pscale_l, "pscale"),
                              (gop, gop_l, "gop"), (goa, goa_l, "goa"),
                              (ident_f, c_ident, "ident_f"), (iota, c_iota, "iota"),
                              (freq, c_freq, "freq"), (maskc, c_mask, "maskc")):
            dma("sp", t_[:], src[:, :], (), [nm], "cres")
        dma("sp", invc[:].rearrange("p g t -> p (g t)"), c_invc[:, :], (), ["invc"], "cres")
        dma("sp", gf_bc[:], g_final.partition_broadcast(128), (), ["gf_bc"], "cres")

        memset(ones_b[:], 1.0, ["ones_b"])
        memset(ones_f[:], 1.0, ["ones_f"])
        cp(ident_b[:], ident_f[:], ["ident_f"], ["ident_b"])
        tt(pg[:], pscale[:], gop[:], ALU.mult, ["pscale", "gop"], ["pg"])
        for rep in range(4):
            cp(wkr[:, :, rep * 32:(rep + 1) * 32], win[:, :, 384:416], ["win"], ["wkr"])
            ts(wkrB[:, :, rep * 32:rep * 32 + 16], win[:, :, 400:416], -1.0, None, ALU.mult, None,
               ["win"], ["wkrB"])
            cp(wkrB[:, :, rep * 32 + 16:rep * 32 + 32], win[:, :, 384:400], ["win"], ["wkrB"])
        for c in range(2):
            wq4 = wuq[:, c, :].rearrange("p (h e) -> p h e", e=96)
            cp(wuqN[:, c, :].rearrange("p (h e) -> p h e", e=64), wq4[:, :, 0:64], ["wuq"], ["wuqN"])
            cp(wuqA[:, c, :].rearrange("p (h e) -> p h e", e=32), wq4[:, :, 64:96], ["wuq"], ["wuqA"])
            wB = wuqB[:, c, :].rearrange("p (h e) -> p h e", e=32)
            ts(wB[:, :, 0:16], wq4[:, :, 80:96], -1.0, None, ALU.mult, None, ["wuq"], ["wuqB"])
            cp(wB[:, :, 16:32], wq4[:, :, 64:80], ["wuq"], ["wuqB"])
        for m in range(4):
            tr(psT[:, m, :], wuk_sb[:, m * 128:(m + 1) * 128], ident_b[:], ["wuk_sb", "ident_b"], ["psT"])
        cp(wukT[:], psT[:, 0:4, :], ["psT"], ["wukT"])
        memset(wuvpad[:], 0.0, ["wuvpad"])
        for half in range(2):
            cp(wuvpad[:].rearrange("p (m t) c -> p m t c", t=2)[:, :, half, half * 64:(half + 1) * 64],
               wuv_sb.rearrange("p (m t) d -> p m t d", t=2)[:, :, half, :],
               ["wuv_sb", "wuvpad"], ["wuvpad"])

        store_ops = []

        def sumsq_tiles(tiles, TP, statcol, src=None):
            lo, hi = statcol + tiles[0], statcol + tiles[-1] + 1
            cols = stat[:TP, lo:hi]
            sr = f"stat:{lo}"
            junk = sq[:].rearrange("p a b -> p (a b)")[:TP, :]
            if USE_ACCUM:
                memset(cols, 0.0, [sr])
                for i in tiles:
                    xin, xr = (src[i][:2] if (src and i in src) else (xres[:TP, i, :], [f"xres:{i}"]))
                    jk = junk.rearrange("p (a b) -> p a b", a=2) if (src and i in src and src[i][2]) else junk
                    act(jk, xin, AF.Square, xr + [sr], ["sq:0", "sq:1", sr],
                        accum=stat[:TP, statcol + i:statcol + i + 1])
            else:
                for i in tiles:
                    jf = ff[:, 6:8, :].rearrange("p a b -> p (a b)")[:TP, 0:D]
                    act(jf, xres[:TP, i, :], AF.Square, [f"xres:{i}"], ["ff:6", "ff:7"])
                    T.add("dve", lambda e, o=stat[:TP, statcol + i:statcol + i + 1], j=jf:
                          e.reduce_sum(out=o, in_=j, axis=mybir.AxisListType.X), ["ff:6", "ff:7"], [sr])
            rstd_chain(cols, cols, 1.0 / D, [sr], sr)
            return sr

        def norm_transpose(NT, TP, TB, gvec, gname, statcol, tiles=None, src=None):
            if tiles is None:
                tiles = list(range(NT))
            if not tiles:
                return
            sr = sumsq_tiles(tiles, TP, statcol, src)
            for i in tiles:
                sc = stat[:TP, statcol + i:statcol + i + 1]
                slot = i % 2
                xin, xr = (src[i][:2] if (src and i in src) else (xres[:TP, i, :], [f"xres:{i}"]))
                xo = xs[:TP, slot, :]
                if src and i in src and src[i][2]:
                    xo = xo.rearrange("p (a b) -> p a b", a=2)
                act(xo, xin, AF.Copy, xr + [sr], [f"xs:{slot}"], scale=sc)
                for c in range(8):
                    tr(psT[:, c, :TP], xs[:TP, slot, c * 128:(c + 1) * 128], ident_b[:TP, :TP],
                       [f"xs:{slot}", "ident_b"], ["psT"])
                for c in range(8):
                    ts(hT[:, c, i * TP:(i + 1) * TP], psT[:, c, :TP], gvec[:, c:c + 1], None, ALU.mult, None,
                       ["psT", gname], [f"hT:{c}"])

        ring_state = {"A": 0, "B": 0}
        rope_state = {}
        stage_pending = {}

        def proj_tokmajor(NT, TP, in_chunks, in_names, w_scr, wname, epilogue, after_pass=None, mid_last=None):
            nf = len(in_chunks)
            for p0 in range(0, NT, 2):
                tiles = list(range(p0, min(p0 + 2, NT)))
                for f in range(nf):
                    s = ring_state["B"] % 8
                    ring_state["B"] += 1
                    dma("sp", ringB[:, s, :], w_scr[f * 128:(f + 1) * 128, :], [wname], [f"rB:{s}"], f"rB:{s}")
                    for ti, i in enumerate(tiles):
                        for n in range(2):
                            bk = ti * 2 + n
                            mm(pb[bk][:TP, :], in_chunks[f][:, i * TP:(i + 1) * TP],
                               ringB[:, s, n * 512:(n + 1) * 512], f == 0, f == nf - 1,
                               [in_names[f], f"rB:{s}"], [PS(bk)])
                if mid_last is not None and p0 + 2 >= NT:
                    mid_last()
                for ti, i in enumerate(tiles):
                    for n in range(2):
                        epilogue(i, n, pb[ti * 2 + n], PS(ti * 2 + n))
                if after_pass is not None:
                    after_pass(tiles)

        def block(x_dram, y_dram, ockv, okr, opool, NT, TP, pos0, kcol0, nk_prev_tiles, causal_blk,
                  first_prompt, write_pool, load_tiles, next_x, pre_done=(), early_next=False, next_pos0=None):
            TB = NT * TP
            kt0 = kcol0 // 128
            kb = kcol0 // 512
            Kw = [f"Klat:{kb}", f"Krope:{kb}", f"V:{kb}"]
            for i in load_tiles:
                dma("pool", xres[:TP, i, :], x_dram[i * TP:(i + 1) * TP, :], (), [f"xres:{i}"], f"ldx:{i}")
            def emit_rope(p0, tb):
                Ct_, St_, tmp = ff[:, 0, :tb], ff[:, 1, :tb], ff[:, 2, :tb]
                ts(tmp, iota[:, :tb], float(p0), None, ALU.add, None, ["iota"], ["ff:2"])
                ts(tmp, tmp, freq[:, 0:1], None, ALU.mult, None, ["ff:2", "freq"], ["ff:2"])

                def sin_table(dst, dname, shift):
                    if shift != 0.0:
                        ts(dst, tmp, shift, None, ALU.add, None, ["ff:2"], [dname])
                        src, sname = dst, dname
                    else:
                        src, sname = tmp, "ff:2"
                    ts(itile[:, :tb], src, 1.0 / (2 * PI), None, ALU.mult, None, [sname], ["itile"])
                    stt(dst, itile[:, :tb], -2 * PI, src, ALU.mult, ALU.add, ["itile", sname], [dname])
                    ts(dst, dst, -PI, PI, ALU.max, ALU.min, [dname], [dname])
                    act(dst, dst, AF.Sin, [dname], [dname])
                sin_table(St_, "ff:1", 0.0)
                sin_table(Ct_, "ff:0", 0.5 * PI)
                rope_state["pos"] = p0
            Ct, St = ff[:, 0, :TB], ff[:, 1, :TB]
            if rope_state.get("pos") != pos0:
                emit_rope(pos0, TB)
            if STAGE < 2:
                return
            norm_transpose(NT, TP, TB, g1, "g1", 0, [i for i in range(NT) if i not in pre_done])
            hr = [f"hT:{c}" for c in range(8)]
            if STAGE < 3:
                return
            if SUB < 19:
                return
            cqT = [bb[:, 18, :TB], bb[:, 19, :TB]]
            rq = ff[:, 3, :TB]
            for m in range(2):
                for c in range(8):
                    mm(pb[m][:, :TB], win[:, c, m * 128:(m + 1) * 128], hT[:, c, :TB], c == 0, c == 7,
                       ["win", hr[c]], [PS(m)])
                ts(cqT[m], pb[m][:, :TB], gq[:, m:m + 1], None, ALU.mult, None, [PS(m), "gq"], [f"bb:{18 + m}"])
                if VV >= 12:
                    act(sq[:, m, :TB], pb[m][:, :TB], AF.Square, [PS(m)], [f"sq:{m}"])
            for c in range(8):
                mm(pb[2][:, :TB], win[:, c, 256:384], hT[:, c, :TB], c == 0, c == 7, ["win", hr[c]], [PS(2)])
            for c in range(8):
                mm(pb[3][:, :TB], wkr[:, c, :], hT[:, c, :TB], c == 0, c == 7, ["wkr", hr[c]], [PS(3)])
            for m in range(2):
                if VV >= 13:
                    mm(pb[4][:, :TB], ones_b[:], sq[:, m, :TB], m == 0, m == 1, ["ones_b", f"sq:{m}"], [PS(4)])
            if VV >= 14:
                rstd_chain(rq, pb[4][:, :TB], 1.0 / QL, [PS(4)], "ff:3")
            if SUB < 20:
                return
            if 2 in stage_pending:
                stage_pending.pop(2)()
            ckvT = ff[:, 4, :TB]
            rkv = ff[:, 5, :TB]
            act(sq[:, 0, :TB], pb[2][:, :TB], AF.Square, [PS(2)], ["sq:0"])
            mm(pb[5][:, :TB], ones_b[:], sq[:, 0, :TB], True, True, ["ones_b", "sq:0"], [PS(5)])
            rstd_chain(rkv, pb[5][:, :TB], 1.0 / R, [PS(5)], "ff:5")
            stt(ckvT, pb[2][:, :TB], gkv[:, 0:1], rkv, ALU.mult, ALU.mult, [PS(2), "gkv", "ff:5"], ["ff:4"])
            act(KlatT[:, kcol0:kcol0 + TB], ckvT, AF.Copy, ["ff:4"], [Kw[0]])
            for i in range(NT):
                tr(pb[6][:TP, i * 128:(i + 1) * 128], ckvT[:, i * TP:(i + 1) * TP], ident_f[:], ["ff:4", "ident_f"],
                   [PS(6)])
            cp(ckv_out[:TP, :NT, :], pb[6][:TP, :NT * 128].rearrange("p (i d) -> p i d", d=128), [PS(6)],
               ["ckv_out"])
            act(Vt[:TP, kt0:kt0 + NT, :], pb[6][:TP, :NT * 128].rearrange("p (i d) -> p i d", d=128), AF.Copy,
                [PS(6)], [Kw[2]])
            store_ops.append(dma("pool", ockv.rearrange("(i p) d -> p i d", p=TP), ckv_out[:TP, :NT, :],
                                 ["ckv_out"], (), "st_ckv"))
            if SUB < 21:
                return
            krT = ff[:, 4, :TB]
            t2 = ff[:, 5, :TB]
            for c in range(8):
                mm(pb[1][:, :TB], wkrB[:, c, :], hT[:, c, :TB], c == 0, c == 7, ["wkrB", hr[c]], [PS(1)])
            tt(krT, pb[3][:, :TB], Ct, ALU.mult, [PS(3), "ff:0"], ["ff:4"])
            tt(t2, pb[1][:, :TB], St, ALU.mult, [PS(1), "ff:1"], ["ff:5"])
            tt(krT, krT, t2, ALU.add, ["ff:4", "ff:5"], ["ff:4"])
            act(KropeT[:, kcol0:kcol0 + TB], krT, AF.Copy, ["ff:4"], [Kw[1]])
            for i in range(NT):
                tr(pb[6][:TP, i * 128:(i + 1) * 128], krT[:, i * TP:(i + 1) * TP], ident_f[:], ["ff:4", "ident_f"],
                   [PS(6)])
            cp(kr_out[:TP, :NT, :], pb[6][:TP, :NT * 128].rearrange("p (i d) -> p i d", d=128)[:, :, 0:DR],
               [PS(6)], ["kr_out"])
            store_ops.append(dma("pool", okr.rearrange("(i p) d -> p i d", p=TP), kr_out[:TP, :NT, :],
                                 ["kr_out"], (), "st_kr"))
            if SUB < 22:
                return
            for i_ in (0, 1):
                if i_ in stage_pending:
                    stage_pending.pop(i_)()
            for g in range(4):
                bk = 2 + (g % 2)
                for c in range(8):
                    mm(pb[bk][:, :TB], win[:, c, 416 + g * 128:416 + (g + 1) * 128], hT[:, c, :TB], c == 0, c == 7,
                       ["win", hr[c]], [PS(bk)])
                act(uext[:, g, EXT:EXT + TB], pb[bk][:, :TB], AF.Copy, [PS(bk)], [f"uext:{g}"])
            if STAGE < 4:
                return
            if 3 in stage_pending:
                stage_pending.pop(3)()
            opT = [bb[:, 10 + g, :TB] for g in range(4)]
            for g, wd in enumerate(WINDOWS):
                ur = f"uext:{g}"
                W = EXT + TB
                cur, curr = uext[:, g, :], ur
                bufs = [(ff[:, 6, :], "ff:6"), (ff[:, 7, :], "ff:7")]
                k, bi = 1, 0
                while k < wd:
                    lo = EXT - (wd - 2 * k)
                    dst, dr = bufs[bi]
                    if POOL_MIX:
                        T.add("pool", lambda e, o=dst[:, lo:W], a_=cur[:, lo:W], b_=cur[:, lo - k:W - k]:
                              e.tensor_tensor(out=o, in0=a_, in1=b_, op=ALU.add), [curr], [dr])
                    else:
                        tt(dst[:, lo:W], cur[:, lo:W], cur[:, lo - k:W - k], ALU.add, [curr], [dr])
                    cur, curr = dst, dr
                    bi ^= 1
                    k *= 2
                pooled = bb[:, g, :TB]
                stt(pooled, cur[:, EXT:W], 1.0 / wd, uext[:, g, EXT:W], ALU.mult, ALU.subtract, [curr, ur],
                    [f"bb:{g}"])
                if first_prompt:
                    t16 = ff[:, 2, 0:16]
                    tt(t16, cur[:, EXT:EXT + 16], invc[:, g, :], ALU.mult, [curr, "invc"], ["ff:2"])
                    tt(pooled[:, 0:16], t16, uext[:, g, EXT:EXT + 16], ALU.subtract, ["ff:2", ur], [f"bb:{g}"])
                bk = g % 2
                mm(pb[bk][:, :TB], wpool[:, g, :], pooled, True, True, ["wpool", f"bb:{g}"], [PS(bk)])
                ts(opT[g], pb[bk][:, :TB], pg[:, g:g + 1], None, ALU.mult, None, [PS(bk), "pg"], [f"bb:{10 + g}"])
                act(sq[:, g % 2, :TB], pb[bk][:, :TB], AF.Square, [PS(bk), "pscale"], [f"sq:{g % 2}"],
                    scale=pscale[:, g:g + 1])
                mm(pb[4][:, :TB], ones_b[:], sq[:, g % 2, :TB], g == 0, g == 3, ["ones_b", f"sq:{g % 2}"], [PS(4)])
            rp = ff[:, 5, :TB]
            rstd_chain(rp, pb[4][:, :TB], 1.0 / PW, [PS(4)], "ff:5")
            for g in range(4):
                tt(opT[g], opT[g], rp, ALU.mult, [f"bb:{10 + g}", "ff:5"], [f"bb:{10 + g}"])
            if write_pool:
                for g in range(4):
                    tr(pb[6][:15, g * 128:(g + 1) * 128], uext[:, g, EXT + TB - 15:EXT + TB], ident_f[:],
                       [f"uext:{g}", "ident_f"], [PS(6)])
                cp(pst[:15, :], pb[6][:15, :], [PS(6)], ["ff:2"])
                store_ops.append(dma("pool", opool[:, :], pst[:15, :], ["ff:2"], (), "st_pool"))
            else:
                cp(uext[:, :, 1:16], uext[:, :, TB + 1:TB + 16], [f"uext:{g}" for g in range(4)],
                   [f"uext:{g}" for g in range(4)])
            if STAGE < 5:
                return
            qn = [bb[:, 14 + m, :TB] for m in range(4)]
            for m in range(4):
                bk = m % 2
                for c in range(2):
                    mm(pb[bk][:, :TB], wuqN[:, c, m * 128:(m + 1) * 128], cqT[c], c == 0, c == 1,
                       ["wuqN", f"bb:{18 + c}"], [PS(bk)])
                tt(qn[m], pb[bk][:, :TB], rq, ALU.mult, [PS(bk), "ff:3"], [f"bb:{14 + m}"])
            batch_heads = not causal_blk
            if batch_heads:
                QlatT = [bb[:, 0, h * TB:(h + 1) * TB] for h in range(NH)]
                QLR = ["bb:0"] * NH
            else:
                QlatT = [bb[:, h, :TB] for h in range(NH)]
                QLR = [f"bb:{h}" for h in range(NH)]
            for h in range(NH):
                m, half = h // 2, h % 2
                bk = 2 + (h % 2)
                mm(pb[bk][:, :TB], wukT[half * 64:(half + 1) * 64, m, :], qn[m][half * 64:(half + 1) * 64, :],
                   True, True, ["wukT", f"bb:{14 + m}"], [PS(bk)])
                act(QlatT[h], pb[bk][:, :TB], AF.Copy, [PS(bk)], [QLR[h]])
            QRC = [8, 9, 20, 21, 22, 23, 24, 25]
            if batch_heads:
                QropeM = [bb[:, 8, h * TB:(h + 1) * TB] for h in range(NH)]
                QRR = ["bb:8"] * NH
            else:
                QropeM = [bb[:, QRC[h], :TB] for h in range(NH)]
                QRR = [f"bb:{QRC[h]}" for h in range(NH)]
            for a in range(2):
                for c in range(2):
                    mm(pb[0][:, :TB], wuqA[:, c, a * 128:(a + 1) * 128], cqT[c], c == 0, c == 1,
                       ["wuqA", f"bb:{18 + c}"], [PS(0)])
                for c in range(2):
                    mm(pb[1][:, :TB], wuqB[:, c, a * 128:(a + 1) * 128], cqT[c], c == 0, c == 1,
                       ["wuqB", f"bb:{18 + c}"], [PS(1)])
                t1, t2 = ff[:, 6, :TB], ff[:, 7, :TB]
                tt(t1, pb[0][:, :TB], Ct, ALU.mult, [PS(0), "ff:0"], ["ff:6"])
                tt(t2, pb[1][:, :TB], St, ALU.mult, [PS(1), "ff:1"], ["ff:7"])
                tt(t1, t1, t2, ALU.add, ["ff:6", "ff:7"], ["ff:6"])
                tt(t1, t1, rq, ALU.mult, ["ff:6", "ff:3"], ["ff:6"])
                for g in range(4):
                    ts(QropeM[4 * a + g], t1, maskc[:, g:g + 1], None, ALU.mult, None, ["ff:6", "maskc"],
                       [QRR[4 * a + g]])
            if STAGE < 6:
                return
            ktiles = []
            for j in range(nk_prev_tiles):
                ktiles.append((j * 128, j, 128, 0, False, j // 4))
            if causal_blk:
                for m in range(NT):
                    ktiles.append((kcol0 + m * 128, kt0 + m, 128, m * 128, True, kb))
            else:
                ktiles.append((kcol0, kt0, TB, 0, False, kb))
            oaT = [bb[:, 14 + m, :TB] for m in range(4)]
            LA = 2
            SB_ = (0, 1, 2)
            NPT = 5
            nk = len(ktiles)
            vheads = [list(range(NH))] if batch_heads else [[h] for h in range(NH)]
            W = TB * len(vheads[0])
            if batch_heads:
                QLv = [bb[:, 0, :W]]
                QRv = [bb[:, 8, :W]]
                QLn, QRn = [["bb:0"]], [["bb:8"]]
            else:
                QLv, QRv = QlatT, QropeM
                QLn, QRn = [[x] for x in QLR], [[x] for x in QRR]
            jobs = [(v, ki) for v in range(len(vheads)) for ki in range(nk)]
            pend = []

            def do_pv(v, ki, psl):
                kc, vt, kr_, n0, diag, kblk = ktiles[ki]
                ob = 3 + (v % 2)
                mm(pb[ob][:, n0:W], Vt[:kr_, vt, :], PT[:kr_, psl, n0:W], ki == 0, ki == nk - 1,
                   [f"V:{kblk}", f"PT:{psl}"], [PS(ob)])

            def epi1(v):
                ob = 3 + (v % 2)
                acc = ff[:, 6 + (v % 2), :W]
                accr = f"ff:{6 + (v % 2)}"
                mm(pb[5][:, :W], ones_f[:], acc, True, False, ["ones_f", accr], [PS(5)])
                mm(pb[5][:, :W], ones_f[:], accP[:, v % 2, :W], False, True, ["ones_f", f"accP:{v % 2}"], [PS(5)])
                rden = ff[:, 4 + (v % 2), :W]
                rdr = f"ff:{4 + (v % 2)}"
                act(rden, pb[5][:, :W], AF.Ln, [PS(5)], [rdr])
                act(rden, rden, AF.Exp, [rdr], [rdr], scale=-1.0)
                olat = sq[:, v % 2, :W]
                tt(olat, pb[ob][:, :W], rden, ALU.mult, [PS(ob), rdr], [f"sq:{v % 2}"])

            def epi2(v):
                for j, h in enumerate(vheads[v]):
                    m = h // 2
                    olat = sq[:, v % 2, j * TB:(j + 1) * TB]
                    mm(pb[6][:, :TB], wuvpad[:, h, :], olat, h % 2 == 0, h % 2 == 1, ["wuvpad", f"sq:{v % 2}"],
                       [PS(6)])
                    if h % 2 == 1:
                        act(oaT[m], pb[6][:, :TB], AF.Copy, [PS(6)], [f"bb:{14 + m}"])

            def flush(upto):
                keep = []
                for (at, fn) in pend:
                    if at <= upto:
                        fn()
                    else:
                        keep.append((at, fn))
                pend[:] = keep

            for idx, (v, ki) in enumerate(jobs):
                kc, vt, kr_, n0, diag, kblk = ktiles[ki]
                sb_ = SB_[idx % len(SB_)]
                psl = idx % NPT
                acc = ff[:, 6 + (v % 2), :W]
                accr = f"ff:{6 + (v % 2)}"
                mm(pb[sb_][:kr_, n0:W], KlatT[:, kc:kc + kr_], QLv[v][:, n0:W], True, False,
                   [f"Klat:{kblk}"] + QLn[v], [PS(sb_)])
                mm(pb[sb_][:kr_, n0:W], KropeT[:, kc:kc + kr_], QRv[v][:, n0:W], False, True,
                   [f"Krope:{kblk}"] + QRn[v], [PS(sb_)])
                act(PT[:kr_, psl, n0:W], pb[sb_][:kr_, n0:W], AF.Exp, [PS(sb_)], [f"PT:{psl}"],
                    scale=SM_SCALE)
                if diag:
                    memset(PT[64:128, psl, n0:n0 + 64], 0.0, [f"PT:{psl}"])
                if ki == 0:
                    cp(acc, PT[:, psl, :W], [f"PT:{psl}"], [accr])
                    memset(accP[:, v % 2, :W], 0.0, [f"accP:{v % 2}"])
                elif ki % 2 == 1:
                    a2 = accP[:, v % 2, :W]
                    tt(a2[:kr_, n0:W], a2[:kr_, n0:W], PT[:kr_, psl, n0:W], ALU.add,
                       [f"accP:{v % 2}", f"PT:{psl}"], [f"accP:{v % 2}"])
                else:
                    tt(acc[:kr_, n0:W], acc[:kr_, n0:W], PT[:kr_, psl, n0:W], ALU.add,
                       [accr, f"PT:{psl}"], [accr])
                pend.append((idx + LA, lambda v=v, ki=ki, psl=psl: do_pv(v, ki, psl)))
                if ki == nk - 1:
                    pend.append((idx + LA + 1, lambda v=v: epi1(v)))
                    pend.append((idx + LA + 4, lambda v=v: epi2(v)))
                flush(idx)
            flush(10 ** 9)
            for m in range(4):
                act(PT[:, m, :TB], oaT[m], AF.Square, [f"bb:{14 + m}"], [f"PT:{m}"])
            for m in range(4):
                mm(pb[2][:, :TB], ones_b[:], PT[:, m, :TB], m == 0, m == 3, ["ones_b", f"PT:{m}"], [PS(2)])
            ra = ff[:, 4, :TB]
            rstd_chain(ra, pb[2][:, :TB], 1.0 / PW, [PS(2)], "ff:4")
            for m in range(4):
                stt(oaT[m], oaT[m], goa[:, m:m + 1], ra, ALU.mult, ALU.mult, [f"bb:{14 + m}", "goa", "ff:4"],
                    [f"bb:{14 + m}"])
            if STAGE < 7:
                return
            def epi_res(i, n, pst_, psr):
                tt(xres[:TP, i, n * 512:(n + 1) * 512], pst_[:TP, :], xres[:TP, i, n * 512:(n + 1) * 512], ALU.add,
                   [psr, f"xres:{i}"], [f"xres:{i}"])
            if NT == 4 and SPLIT_NORM2:
                proj_tokmajor(NT, TP, oaT + opT,
                              [f"bb:{14 + m}" for m in range(4)] + [f"bb:{10 + g}" for g in range(4)],
                              s_wo, "s_wo", epi_res, None,
                              lambda: norm_transpose(NT, TP, TB, g2, "g2", 4, [0, 1]))
                n2_tiles = [2, 3]
            else:
                proj_tokmajor(NT, TP, oaT + opT,
                              [f"bb:{14 + m}" for m in range(4)] + [f"bb:{10 + g}" for g in range(4)],
                              s_wo, "s_wo", epi_res)
                n2_tiles = None
            if STAGE < 8:
                return
            norm_transpose(NT, TP, TB, g2, "g2", 4, n2_tiles)
            if STAGE < 9:
                return
            actT = [bb[:, f, :TB] for f in range(NF)]
            use_stage = bool(STAGE_X and early_next and next_x is not None and NT == 4)
            stgA = ff[:, 4:6, :].rearrange("p a b -> p (a b)")[:, 0:D]
            stgB = ff[:, 6:8, :].rearrange("p a b -> p (a b)")[:, 0:D]
            stg_src = {0: (uext[:, 0:2, EXT:EXT + 512], ["uext:0", "uext:1"], True),
                       1: (uext[:, 2:4, EXT:EXT + 512], ["uext:2", "uext:3"], True),
                       2: (stgA, ["ff:4", "ff:5"], False), 3: (stgB, ["ff:6", "ff:7"], False)}
            if EARLY_ROPE and next_pos0 is not None:
                emit_rope(next_pos0, 512)
            if use_stage:
                dma("pool", stgA, next_x[256:384, :], (), ["ff:4", "ff:5"], "ldxs:2")
                for i in (0, 1):
                    dma("pool", stg_src[i][0],
                        next_x[i * 128:(i + 1) * 128, :].rearrange("p (a b) -> p a b", a=2), (), stg_src[i][1],
                        f"ldxs:{i}")
            for f in range(NF):
                s = ring_state["A"] % 3
                ring_state["A"] += 1
                dma("sp", ringA[:, s, :], s_gu[f * 128:(f + 1) * 128, :], ["s_gu"],
                    [f"rA:{s}"] + {0: ["wuq", "wuk_sb"], 1: ["wuv_sb"]}.get(s, []), f"rA:{s}")
                wv = ringA[:, s, :].rearrange("p (t c n) -> p t c n", t=2, c=8)
                bg, bu = (f % 2) * 2, (f % 2) * 2 + 1
                for c in range(8):
                    mm(pb[bg][:, :TB], wv[:, 0, c, :], hT[:, c, :TB], c == 0, c == 7, [f"rA:{s}", hr[c]], [PS(bg)])
                for c in range(8):
                    mm(pb[bu][:, :TB], wv[:, 1, c, :], hT[:, c, :TB], c == 0, c == 7, [f"rA:{s}", hr[c]], [PS(bu)])
                sg = ff[:, 6 + (f % 2), :TB]
                sgr = f"ff:{6 + (f % 2)}"
                act(sg, pb[bg][:, :TB], AF.Silu, [PS(bg)], [sgr])
                tt(actT[f], sg, pb[bu][:, :TB], ALU.mult, [sgr, PS(bu)], [f"bb:{f}"])
            if use_stage:
                dma("pool", stgB, next_x[384:512, :], (), ["ff:6", "ff:7"], "ldxs:3")

            def final_tiles(tiles):
                sr = sumsq_tiles(tiles, TP, 8)
                for i in tiles:
                    sc = stat[:TP, 8 + i:9 + i]
                    stt(xres[:TP, i, :], xres[:TP, i, :], sc, gf_bc[:TP, :], ALU.mult, ALU.mult,
                        [f"xres:{i}", sr, "gf_bc"], [f"xres:{i}"])
                    store_ops.append(dma("pool", y_dram[i * TP:(i + 1) * TP, :], xres[:TP, i, :], [f"xres:{i}"], (),
                                         f"sty:{i}"))
                    if next_x is not None:
                        if use_stage:
                            def _cp(i=i):
                                xo = xres[:, i, :]
                                if stg_src[i][2]:
                                    xo = xo.rearrange("p (a b) -> p a b", a=2)
                                act(xo, stg_src[i][0], AF.Copy, stg_src[i][1], [f"xres:{i}"])
                            stage_pending[i] = _cp
                        else:
                            dma("pool", xres[:, i, :], next_x[i * 128:(i + 1) * 128, :], (), [f"xres:{i}"],
                                f"ldx:{i}")
            if STAGE < 10:
                return
            def early_norm():
                if use_stage:
                    norm_transpose(4, 128, 512, g1, "g1", 0, [0, 1, 2, 3], stg_src)
                else:
                    norm_transpose(4, 128, 512, g1, "g1", 0, [0, 1])
            proj_tokmajor(NT, TP, actT, [f"bb:{f}" for f in range(NF)], s_down, "s_down", epi_res, final_tiles,
                          early_norm if (early_next and next_x is not None and NT == 4) else None)

        if DO_SAMPLE and STAGE >= 1:
            kres = [f"Klat:{b}" for b in range(8)]
            rres = [f"Krope:{b}" for b in range(8)]
            vres = [f"V:{b}" for b in range(8)]
            stg = ff[:, 6:8, :].rearrange("p a b -> p (a b)")[:, 0:1024]
            for q in range(4):
                dma("sp", stg.rearrange("p (t r) -> p t r", r=128),
                    cc_kv[q * 1024:(q + 1) * 1024, :].rearrange("(t p) r -> p t r", p=128), (), ["ff:6", "ff:7"],
                    "ldc")
                act(Vt[:, q * 8:(q + 1) * 8, :], stg.rearrange("p (t r) -> p t r", r=128), AF.Copy,
                    ["ff:6", "ff:7"], [vres[2 * q], vres[2 * q + 1]])
            krst = bb[:, 0:8, :].rearrange("p a (t q) -> p (a t) q", q=128)
            stg2 = ff[:, 4:6, :].rearrange("p a b -> p (a b)")[:, 0:1024].rearrange("p (t d) -> p t d", d=32)
            for q in range(4):
                dma("sp", stg2[:, q * 8:(q + 1) * 8, :],
                    cc_kr[q * 1024:(q + 1) * 1024, :].rearrange("(t p) d -> p t d", p=128), (), ["ff:4", "ff:5"],
                    f"ldk:{q}")
            for rep in range(4):
                cp(krst[:, :, rep * 32:(rep + 1) * 32], stg2, ["ff:4", "ff:5"], [f"bb:{i}" for i in range(8)])
            dma("sp", spsb[:15, :], st_pool[:, :], (), ["ff:3"], "ldsp")
            for t8 in range(4 if SUB >= 1 else 0):
                for j in range(8):
                    t = t8 * 8 + j
                    tr(psT[:, j, :], Vt[:, t, :], ident_b[:], [f"V:{t // 4}", "ident_b"], ["psT"])
                cp(KlatT[:, t8 * 1024:(t8 + 1) * 1024], psT[:].rearrange("p j k -> p (j k)"), ["psT"],
                   [kres[2 * t8], kres[2 * t8 + 1]])
            for t8 in range(4 if SUB >= 2 else 0):
                for j in range(8):
                    t = t8 * 8 + j
                    tr(psT[:, j, :], krst[:, t, :], ident_b[:], [f"bb:{t // 4}", "ident_b"], ["psT"])
                cp(KropeT[:, t8 * 1024:(t8 + 1) * 1024], psT[:].rearrange("p j k -> p (j k)"), ["psT"],
                   [rres[2 * t8], rres[2 * t8 + 1]])
            for g in range(4 if SUB >= 3 else 0):
                tr(pb[6][:, g * 128:g * 128 + 15], spsb[:15, g * 128:(g + 1) * 128], ident_f[:15, :15],
                   ["ff:3", "ident_f"], [PS(6)])
            cp(uext[:, :, 1:16], pb[6][:, :].rearrange("p (g k) -> p g k", k=128)[:, :, 0:15], [PS(6)],
               [f"uext:{g}" for g in range(4)])
            for i in range(1, 4):
                dma("pool", xres[:, i, :], xp[i * 128:(i + 1) * 128, :], (), [f"xres:{i}"], f"ldx:{i}")
            block(xsm, y_s, o_ckv_s, o_kr_s, o_pool_s, 1, DEC, PAST, PAST, 32, False, False, True, [0],
                  xp[0:512, :] if NBLK > 0 else None, next_pos0=(0 if NBLK > 0 else None))
            first_loads = []
        else:
            first_loads = [0, 1, 2, 3]

        memset(uext[:, :, 0:16], 0.0, [f"uext:{g}" for g in range(4)])
        for J in range(NBLK):
            block(xp[J * 512:(J + 1) * 512, :], y_p[J * 512:(J + 1) * 512, :],
                  o_ckv_p[J * 512:(J + 1) * 512, :], o_kr_p[J * 512:(J + 1) * 512, :], o_pool_p,
                  4, 128, J * 512, J * 512, 4 * J, True, J == 0, J == SEQ // 512 - 1,
                  first_loads if J == 0 else [],
                  xp[(J + 1) * 512:(J + 2) * 512, :] if J + 1 < NBLK else None,
                  pre_done=(((0, 1, 2, 3) if STAGE_X else (0, 1)) if (J > 0 and EARLY_NORM) else ()),
                  early_next=bool(EARLY_NORM), next_pos0=((J + 1) * 512 if J + 1 < NBLK else None))

        lastst = {}
        for op in store_ops:
            lastst[op.dkey] = op
        T.add("sp", None, (), (), extra=list(lastst.values()))

        dkeys = T.resolve()
        esem = {e: es.enter_context(nc.semaphore(f"e_{e}")) for e in Tracker.ENGS}
        dsem = {k: es.enter_context(nc.semaphore("d_" + k.replace(":", "_"))) for k in dkeys}
        with nc.Block() as blk:
            @blk.tensor
            def _(e):
                T.emit("pe", e, esem, dsem)

            @blk.scalar
            def _(e):
                T.emit("act", e, esem, dsem)

            @blk.vector
            def _(e):
                T.emit("dve", e, esem, dsem)

            @blk.gpsimd
            def _(e):
                T.emit("pool", e, esem, dsem)

            @blk.sync
            def _(e):
                T.emit("sp", e, esem, dsem)
    return nc


def _chunked(v, n):
    return np.ascontiguousarray(np.asarray(v, np.float32).reshape(n, 128).T)


def kernel(x_prompt, x_sample, cache_kv_latent, cache_k_rope, state_pool,
           g_norm1, w_in, g_q, w_uq, g_kv, w_uk, w_uv, w_pool, pool_scale,
           g_out_attn, g_out_pool, w_o, g_norm2, w_gate, w_up, w_down, g_final):
    f = lambda a: np.ascontiguousarray(np.asarray(a, np.float32))
    w_in0, w_uq0 = f(w_in)[0], f(w_uq)[0]
    shared = {
        "w_in_l": f(w_in0.reshape(8, 128, INW).transpose(1, 0, 2).reshape(128, 8 * INW)),
        "w_uq_l": f(w_uq0.reshape(2, 128, 768).transpose(1, 0, 2).reshape(128, 2 * 768)),
        "w_uk_l": f(f(w_uk)[0].transpose(1, 0, 2).reshape(128, NH * DN)),
        "w_uv_l": f(f(w_uv)[0].transpose(1, 0, 2).reshape(128, NH * DN)),
        "w_pool_l": f(f(w_pool)[0].transpose(1, 0, 2).reshape(128, 4 * 128)),
        "w_down_l": f(f(w_down)[0]),
        "w_o_l": f(f(w_o)[0]),
        "g1_l": _chunked(f(g_norm1)[0], 8), "g2_l": _chunked(f(g_norm2)[0], 8),
        "gq_l": _chunked(f(g_q)[0], 2), "gkv_l": _chunked(f(g_kv)[0], 1),
        "pscale_l": _chunked(f(pool_scale)[0], 4), "gop_l": _chunked(f(g_out_pool)[0], 4),
        "goa_l": _chunked(f(g_out_attn)[0], 4),
        "g_final": f(g_final),
    }
    wg = f(w_gate)[0].reshape(8, 128, NF, 128).transpose(2, 1, 0, 3)
    wu = f(w_up)[0].reshape(8, 128, NF, 128).transpose(2, 1, 0, 3)
    shared["w_gu_l"] = f(np.stack([wg, wu], axis=2).reshape(NF * 128, 2 * 8 * 128))
    shared["c_ident"] = np.eye(128, dtype=np.float32)
    shared["c_iota"] = np.ascontiguousarray(np.broadcast_to(np.arange(512, dtype=np.float32), (128, 512)))
    fr = np.power(np.float32(10000.0), -np.arange(0, DR, 2, dtype=np.float32) / np.float32(DR)).astype(np.float32)
    shared["c_freq"] = np.ascontiguousarray(fr[np.arange(128) % 16].reshape(128, 1))
    invc = np.zeros((4, 16), np.float32)
    for g, wd in enumerate(WINDOWS):
        invc[g] = 1.0 / np.minimum(np.arange(16) + 1, wd).astype(np.float32)
    shared["c_invc"] = np.ascontiguousarray(np.broadcast_to(invc.reshape(1, 64), (128, 64)))
    shared["c_mask"] = np.ascontiguousarray((np.arange(128)[:, None] // 32 == np.arange(4)[None, :]).astype(np.float32))

    xp_, xs_ = f(x_prompt), f(x_sample)
    ckv_, ckr_, stp_ = f(cache_kv_latent)[0], f(cache_k_rope)[0], f(state_pool)[0]
    in_maps = []
    for c in range(NCORES):
        m = dict(shared)
        m.update({"xp": xp_[c], "xsm": xs_[c], "cc_kv": ckv_[c], "cc_kr": ckr_[c], "st_pool": stp_[c]})
        in_maps.append(m)
    nc = build_program()
    res = run_bass_kernel_spmd(nc, in_maps, core_ids=list(range(NCORES)))
    rs = res.results
    st = lambda k: np.stack([np.asarray(rs[c % NCORES][k], np.float32) for c in range(8)], axis=0)
    return (st("y_p"), st("y_s"), st("o_ckv_p")[None], st("o_kr_p")[None], st("o_pool_p")[None],
            st("o_ckv_s")[None], st("o_kr_s")[None], st("o_pool_s")[None])
```
